# Optimizing a Trainium2 kernel written in Bass

```python
import math
import jax, jax.numpy as jnp
from jax import lax
import numpy as np

D_MODEL = 1024
BATCH = 16
SEQ = 4096
DEPTH = 2
DEC_BATCH = 2
DEC_SEQ = 8192
PAST_LEN = 128

D_FF = 2816
MIX_WIDTH = D_MODEL
RG_WIDTH = D_MODEL // 4
RG_HEADS = 4
RG_BW = RG_WIDTH // RG_HEADS
RG_C = 8.0
CONV_W = 4
CONV_LEFT = 2
ATT_HEADS = 4
HEAD_DIM = D_MODEL // 16
ATT_WIDTH = ATT_HEADS * 2 * HEAD_DIM
F_WIDTH = MIX_WIDTH - RG_WIDTH - ATT_WIDTH
F_GROUPS = 4
F_GW = F_WIDTH // F_GROUPS
Q_BLOCK = 128
SPLITS = [RG_WIDTH, 2 * RG_WIDTH, 2 * RG_WIDTH + ATT_WIDTH, 2 * RG_WIDTH + 2 * ATT_WIDTH, 2 * RG_WIDTH + 3 * ATT_WIDTH]
V_OFF = 2 * RG_WIDTH + 2 * ATT_WIDTH
IN_WIDTH = 2 * RG_WIDTH + 3 * ATT_WIDTH + F_WIDTH
ALPHA = (2.0 * DEPTH) ** 0.25
BETA = (8.0 * DEPTH) ** -0.25
LN_EPS = 1e-5
NORM_EPS = 1e-5

kernel_name = 'hymba_rglru_diffattn_fnet_macaron_encoder'


def layer_norm(x, g, b):
    xf = x.astype(jnp.float32)
    mu = jnp.mean(xf, -1, keepdims=True)
    xc = xf - mu
    var = jnp.mean(xc * xc, -1, keepdims=True)
    return (xc * lax.rsqrt(var + LN_EPS) * g.astype(jnp.float32) + b.astype(jnp.float32)).astype(x.dtype)


def swiglu(x, wg, wu, wd):
    hg = jnp.einsum('bsd,df->bsf', x, wg)
    hu = jnp.einsum('bsd,df->bsf', x, wu)
    return jnp.einsum('bsf,fd->bsd', jax.nn.silu(hg) * hu, wd)


def centred_dwconv(x, w, b):
    S = x.shape[1]
    xp = jnp.pad(x, ((0, 0), (CONV_LEFT, CONV_W - 1 - CONV_LEFT), (0, 0)))
    y = b
    for j in range(CONV_W):
        y = y + xp[:, j:j + S, :] * w[j]
    return y


def linear_scan(a, u):
    def combine(left, right):
        a_l, u_l = left
        a_r, u_r = right
        return a_l * a_r, a_r * u_l + u_r
    _, h = lax.associative_scan(combine, (a, u), axis=1)
    return h


def rglru_direction(x, wa, ba, wx, bx, lam):
    B, S, _ = x.shape
    xh = x.reshape(B, S, RG_HEADS, RG_BW)
    r = jax.nn.sigmoid(jnp.einsum('bshi,hij->bshj', xh, wa.astype(jnp.float32)).reshape(B, S, RG_WIDTH) + ba.astype(jnp.float32))
    i = jax.nn.sigmoid(jnp.einsum('bshi,hij->bshj', xh, wx.astype(jnp.float32)).reshape(B, S, RG_WIDTH) + bx.astype(jnp.float32))
    log_a = -RG_C * jax.nn.softplus(-lam.astype(jnp.float32)) * r
    a = jnp.exp(log_a)
    mult = jnp.sqrt(-jnp.expm1(2.0 * log_a))
    return linear_scan(a, mult * (i * x))


def bidir_rglru(x, wa, ba, wx, bx, lam):
    fwd = rglru_direction(x, wa[0], ba[0], wx[0], bx[0], lam[0])
    bwd = jnp.flip(rglru_direction(jnp.flip(x, 1), wa[1], ba[1], wx[1], bx[1], lam[1]), 1)
    return fwd + bwd


def diff_attention(q, k, v, lam, subln_g, lam_init):
    B, S = q.shape[0], q.shape[1]
    nb = S // Q_BLOCK
    slopes = 2.0 ** (-8.0 * jnp.arange(1, ATT_HEADS + 1, dtype=jnp.float32) / ATT_HEADS)
    scale = HEAD_DIM ** -0.5
    pos_k = jnp.arange(S, dtype=jnp.int32)
    qb = jnp.moveaxis(q.reshape(B, nb, Q_BLOCK, ATT_HEADS, 2, HEAD_DIM), 1, 0)
    starts = jnp.arange(nb, dtype=jnp.int32) * Q_BLOCK

    def block(args):
        qi, start = args
        pos_q = start + jnp.arange(Q_BLOCK, dtype=jnp.int32)
        dist = jnp.abs(pos_q[:, None] - pos_k[None, :]).astype(jnp.float32)
        bias = -slopes[:, None, None] * dist
        s = jnp.einsum('bqhcd,bkhcd->bhcqk', qi, k, preferred_element_type=jnp.float32) * scale + bias[None, :, None]
        p = jax.nn.softmax(s, axis=-1)
        w = p[:, :, 0] - lam * p[:, :, 1]
        return jnp.einsum('bhqk,bkhe->bqhe', w.astype(v.dtype), v, preferred_element_type=jnp.float32)

    o = lax.map(block, (qb, starts))
    o = jnp.moveaxis(o, 0, 1).reshape(B, S, ATT_HEADS, 2 * HEAD_DIM)
    o = o * lax.rsqrt(jnp.mean(o * o, -1, keepdims=True) + NORM_EPS) * subln_g.astype(jnp.float32) * (1.0 - lam_init)
    return o.reshape(B, S, ATT_WIDTH)


def fourier_mix(f):
    B, S, _ = f.shape
    fg = f.astype(jnp.float32).reshape(B, S, F_GROUPS, F_GW)
    out = jnp.fft.fft2(fg, axes=(1, 3), norm='ortho').real
    return out.reshape(B, S, F_WIDTH)


def hybrid_mixer(h, w_in, conv_w, conv_b, rg_wa, rg_ba, rg_wx, rg_bx, rg_lambda, lambda_qk, subln_g, w_out, lam_init):
    B, S, _ = h.shape
    proj = jnp.einsum('bsd,de->bse', h, w_in)
    rx, rgate, q, k, v, fx = jnp.split(proj, SPLITS, axis=-1)
    xc = centred_dwconv(rx, conv_w, conv_b).astype(jnp.float32)
    out_a = jax.nn.gelu(rgate.astype(jnp.float32)) * bidir_rglru(xc, rg_wa, rg_ba, rg_wx, rg_bx, rg_lambda)
    lq = lambda_qk.astype(jnp.float32)
    lam = jnp.exp(jnp.sum(lq[0] * lq[1])) - jnp.exp(jnp.sum(lq[2] * lq[3])) + lam_init
    out_b = diff_attention(q.reshape(B, S, ATT_HEADS, 2, HEAD_DIM), k.reshape(B, S, ATT_HEADS, 2, HEAD_DIM),
                           v.reshape(B, S, ATT_HEADS, 2 * HEAD_DIM), lam, subln_g, lam_init)
    out_c = fourier_mix(fx)
    y = jnp.concatenate([out_a, out_b, out_c], axis=-1).astype(h.dtype)
    return jnp.einsum('bse,ed->bsd', y, w_out)


def encoder_trunk(x, ln_g, ln_b, ffn1_wg, ffn1_wu, ffn1_wd, ffn2_wg, ffn2_wu, ffn2_wd, w_in, conv_w, conv_b,
                  rg_wa, rg_ba, rg_wx, rg_bx, rg_lambda, lambda_qk, subln_g, w_out):
    for l in range(DEPTH):
        lam_init = 0.8 - 0.6 * math.exp(-0.3 * l)
        x = layer_norm(ALPHA * x + 0.5 * swiglu(x, ffn1_wg[l], ffn1_wu[l], ffn1_wd[l]), ln_g[l, 0], ln_b[l, 0])
        x = layer_norm(ALPHA * x + hybrid_mixer(x, w_in[l], conv_w[l], conv_b[l], rg_wa[l], rg_ba[l], rg_wx[l], rg_bx[l],
                                                rg_lambda[l], lambda_qk[l], subln_g[l], w_out[l], lam_init),
                       ln_g[l, 1], ln_b[l, 1])
        x = layer_norm(ALPHA * x + 0.5 * swiglu(x, ffn2_wg[l], ffn2_wu[l], ffn2_wd[l]), ln_g[l, 2], ln_b[l, 2])
    return x


def setup_inputs(seed: int = 0) -> dict:
    key = jax.random.key(seed)
    ks = jax.random.split(key, 24)
    f32 = jnp.float32

    def nrm(k, shape, scale):
        return jax.random.normal(k, shape, f32) * scale

    x_prompt = nrm(ks[0], (BATCH, SEQ, D_MODEL), 1.0)
    x_sample = nrm(ks[1], (DEC_BATCH, DEC_SEQ, D_MODEL), 1.0)
    ln_g = 1.0 + nrm(ks[2], (DEPTH, 3, D_MODEL), 0.02)
    ln_b = nrm(ks[3], (DEPTH, 3, D_MODEL), 0.02)
    ffn1_wg = nrm(ks[4], (DEPTH, D_MODEL, D_FF), D_MODEL ** -0.5)
    ffn1_wu = nrm(ks[5], (DEPTH, D_MODEL, D_FF), D_MODEL ** -0.5)
    ffn1_wd = nrm(ks[6], (DEPTH, D_FF, D_MODEL), D_FF ** -0.5 * BETA)
    ffn2_wg = nrm(ks[7], (DEPTH, D_MODEL, D_FF), D_MODEL ** -0.5)
    ffn2_wu = nrm(ks[8], (DEPTH, D_MODEL, D_FF), D_MODEL ** -0.5)
    ffn2_wd = nrm(ks[9], (DEPTH, D_FF, D_MODEL), D_FF ** -0.5 * BETA)
    col_scale = jnp.ones((IN_WIDTH,), f32).at[V_OFF:V_OFF + ATT_WIDTH].set(BETA)
    w_in = nrm(ks[10], (DEPTH, D_MODEL, IN_WIDTH), D_MODEL ** -0.5) * col_scale
    conv_w = nrm(ks[11], (DEPTH, CONV_W, RG_WIDTH), CONV_W ** -0.5)
    conv_b = nrm(ks[12], (DEPTH, RG_WIDTH), 0.01)
    rg_wa = nrm(ks[13], (DEPTH, 2, RG_HEADS, RG_BW, RG_BW), RG_BW ** -0.5)
    rg_ba = nrm(ks[14], (DEPTH, 2, RG_WIDTH), 0.01)
    rg_wx = nrm(ks[15], (DEPTH, 2, RG_HEADS, RG_BW, RG_BW), RG_BW ** -0.5)
    rg_bx = nrm(ks[16], (DEPTH, 2, RG_WIDTH), 0.01)
    a_target = jax.random.uniform(ks[17], (DEPTH, 2, RG_WIDTH), f32, minval=0.9, maxval=0.999)
    base = a_target ** (1.0 / RG_C)
    rg_lambda = jnp.log(base) - jnp.log1p(-base)
    lambda_qk = nrm(ks[18], (DEPTH, 4, HEAD_DIM), 0.1)
    subln_g = 1.0 + nrm(ks[19], (DEPTH, 2 * HEAD_DIM), 0.02)
    w_out = nrm(ks[20], (DEPTH, MIX_WIDTH, D_MODEL), MIX_WIDTH ** -0.5 * BETA)
    return {'x_prompt': x_prompt, 'x_sample': x_sample, 'ln_g': ln_g, 'ln_b': ln_b,
            'ffn1_wg': ffn1_wg, 'ffn1_wu': ffn1_wu, 'ffn1_wd': ffn1_wd,
            'ffn2_wg': ffn2_wg, 'ffn2_wu': ffn2_wu, 'ffn2_wd': ffn2_wd,
            'w_in': w_in, 'conv_w': conv_w, 'conv_b': conv_b,
            'rg_wa': rg_wa, 'rg_ba': rg_ba, 'rg_wx': rg_wx, 'rg_bx': rg_bx, 'rg_lambda': rg_lambda,
            'lambda_qk': lambda_qk, 'subln_g': subln_g, 'w_out': w_out}


def reference(x_prompt, x_sample, ln_g, ln_b, ffn1_wg, ffn1_wu, ffn1_wd, ffn2_wg, ffn2_wu, ffn2_wd, w_in, conv_w, conv_b,
              rg_wa, rg_ba, rg_wx, rg_bx, rg_lambda, lambda_qk, subln_g, w_out):
    y_prompt = encoder_trunk(x_prompt, ln_g, ln_b, ffn1_wg, ffn1_wu, ffn1_wd, ffn2_wg, ffn2_wu, ffn2_wd, w_in, conv_w, conv_b,
                             rg_wa, rg_ba, rg_wx, rg_bx, rg_lambda, lambda_qk, subln_g, w_out)
    y_sample = encoder_trunk(x_sample, ln_g, ln_b, ffn1_wg, ffn1_wu, ffn1_wd, ffn2_wg, ffn2_wu, ffn2_wd, w_in, conv_w, conv_b,
                             rg_wa, rg_ba, rg_wx, rg_bx, rg_lambda, lambda_qk, subln_g, w_out)
    return (y_prompt, y_sample)
```

```python
import math
from contextlib import ExitStack
import numpy as np
import ml_dtypes
import concourse.bass as bass
import concourse.mybir as mybir
from concourse.bass_utils import run_bass_kernel_spmd

F32, BF16 = mybir.dt.float32, mybir.dt.bfloat16
ALU = mybir.AluOpType
AF = mybir.ActivationFunctionType

D = 1024
DFF = 2816
NF = DFF // 128
DEPTH = 2
T = 512
ALPHA = (2.0 * DEPTH) ** 0.25
LN_EPS = 1e-5
NORM_EPS = 1e-5
RG_C = 8.0
BG_BATCH = 8
MAGIC = 12582912.0
SLOPES = [2.0 ** (-8.0 * (h + 1) / 4) for h in range(4)]
C_GU1, C_GU2, C_IN, C_OUT, NCH8 = 0, 44, 88, 102, 110
SM_LNG = 0
SM_LNB = 48
SM_CONVW = 96
SM_CONVB = 112
SM_BA = 116
SM_BX = 124
SM_LAM = 132
SM_SUBG = 140
SM_LQK = 142
NSMALL = 142 + 512


class Buf:
    __slots__ = ("t", "lw", "rd", "excl")

    def __init__(self, t):
        self.t = t
        self.lw = None
        self.rd = {}
        self.excl = False

    def __getitem__(self, k):
        return self.t[k]


class PBuf(Buf):
    __slots__ = ("i",)

    def __init__(self, P, t, i):
        Buf.__init__(self, t)
        self.i = i
        self.excl = True
        P.bufs.append(self)

    def __getitem__(self, k):
        assert isinstance(k, slice) and k == slice(None), k
        return self.t[:, self.i * 512:(self.i + 1) * 512]

    def part(self, p0, p1):
        return self.t[p0:p1, self.i * 512:(self.i + 1) * 512]


class Prog:
    ENG = ("pe", "act", "dve", "pool", "sp")

    def __init__(self, nc, ndma=24):
        self.nc = nc
        self.ops = {e: [] for e in self.ENG}
        self.cnt = {e: 0 for e in self.ENG}
        self.seen = {e: {} for e in self.ENG}
        self.bufs = []
        self.ndma = ndma
        self.dtot = [0] * ndma
        self.drr = 0
        self.sb_off = 16512
        self.n_alloc = 0

    def sb(self, name, shape, dtype):
        esz = 2 if dtype == BF16 else 4
        n = 1
        for s in shape[1:]:
            n *= s
        nbytes = (n * esz + 63) // 64 * 64
        self.n_alloc += 1
        t = self.nc.alloc_sbuf_tensor_at(f"{name}_{self.n_alloc}", list(shape), dtype, offset=self.sb_off)
        self.sb_off += nbytes
        assert self.sb_off <= 228000, f"SBUF overflow at {name}: {self.sb_off}"
        b = Buf(t)
        self.bufs.append(b)
        return b

    def wrap(self, t):
        b = Buf(t)
        self.bufs.append(b)
        return b

    def _need(self, eng, tok, waits):
        if tok is None:
            return
        if tok[0] == "c":
            key, val = tok[1], tok[2]
            if key == eng and eng == "pe":
                return
        else:
            key, val = ("d", tok[1]), tok[2]
        if self.seen[eng].get(key, 0) >= val:
            return
        if waits.get(key, 0) < val:
            waits[key] = val

    def op(self, eng, fn, r=(), w=(), dma=False):
        waits = {}
        for b in r:
            self._need(eng, b.lw, waits)
            if b.excl:
                for k_, tok in b.rd.items():
                    if k_ != eng:
                        self._need(eng, tok, waits)
        for b in w:
            self._need(eng, b.lw, waits)
            for tok in b.rd.values():
                self._need(eng, tok, waits)
        if dma:
            j = self.drr
            self.drr = (self.drr + 1) % self.ndma
            if self.dtot[j] > 0:
                self._need(eng, ("d", j, self.dtot[j]), waits)
            self.dtot[j] += 16
            tok = ("d", j, self.dtot[j])
            key = ("d", j)
        else:
            self.cnt[eng] += 1
            tok = ("c", eng, self.cnt[eng])
            key = eng
        for k, v in waits.items():
            self.seen[eng][k] = v
        self.ops[eng].append((fn, tuple(waits.items()), key))
        for b in r:
            b.rd[key] = tok
        for b in w:
            b.lw = tok
            b.rd = {}

    def barrier(self):
        for eng in self.ENG:
            waits = {}
            for e2 in self.ENG:
                if e2 != eng and self.cnt[e2] > 0:
                    self._need(eng, ("c", e2, self.cnt[e2]), waits)
            for j in range(self.ndma):
                if self.dtot[j] > 0:
                    self._need(eng, ("d", j, self.dtot[j]), waits)
            for k, v in waits.items():
                self.seen[eng][k] = v
            self.ops[eng].append((None, tuple(waits.items()), None))
        for b in self.bufs:
            b.lw = None
            b.rd = {}

    def emit(self):
        nc = self.nc
        with ExitStack() as st:
            csem = {e: st.enter_context(nc.semaphore(f"c_{e}")) for e in self.ENG}
            dsem = [st.enter_context(nc.semaphore(f"d_{j}")) for j in range(self.ndma)]
            block = st.enter_context(nc.Block())

            def run(name):
                ops = self.ops[name]

                def f(e):
                    for fn, waits, key in ops:
                        for k, v in waits:
                            e.wait_ge(dsem[k[1]] if isinstance(k, tuple) else csem[k], v)
                        if fn is None:
                            continue
                        ins = fn(e)
                        if isinstance(key, tuple):
                            ins.then_inc(dsem[key[1]], 16)
                        else:
                            ins.then_inc(csem[key], 1)
                return f

            block.tensor(run("pe"))
            block.scalar(run("act"))
            block.vector(run("dve"))
            block.gpsimd(run("pool"))
            block.sync(run("sp"))


def MM(P, psb, out, lhsT, rhs, start, stop, rd):
    P.op("pe", lambda e: e.matmul(out, lhsT, rhs, start=start, stop=stop), r=rd, w=[psb])


def ACTV(P, out, in_, func, r, w, bias=0.0, scale=1.0):
    P.op("act", lambda e: e.activation(out=out, in_=in_, func=func, bias=bias, scale=scale), r=r, w=w)


def TT(P, eng, out, in0, in1, op, r, w):
    P.op(eng, lambda e: e.tensor_tensor(out=out, in0=in0, in1=in1, op=op), r=r, w=w)


def TS(P, eng, out, in0, s1, s2, op0, op1, r, w):
    if s2 is None:
        P.op(eng, lambda e: e.tensor_scalar(out=out, in0=in0, scalar1=s1, scalar2=None, op0=op0), r=r, w=w)
    else:
        P.op(eng, lambda e: e.tensor_scalar(out=out, in0=in0, scalar1=s1, scalar2=s2, op0=op0, op1=op1), r=r, w=w)


def STT(P, eng, out, in0, scalar, in1, op0, op1, r, w):
    eng = "dve"
    P.op(eng, lambda e: e.scalar_tensor_tensor(out=out, in0=in0, scalar=scalar, in1=in1, op0=op0, op1=op1), r=r, w=w)


def CP(P, eng, out, in_, r, w):
    if eng == "act":
        P.op("act", lambda e: e.activation(out=out, in_=in_, func=AF.Copy), r=r, w=w)
    else:
        P.op(eng, lambda e: e.tensor_copy(out=out, in_=in_), r=r, w=w)


def RECIP(P, out, in_, r, w):
    P.op("dve", lambda e: e.reciprocal(out=out, in_=in_), r=r, w=w)


def DMA(P, out, in_, r=(), w=(), eng="sp"):
    P.op(eng, lambda e: e.dma_start(out=out, in_=in_), r=r, w=w, dma=True)


def MSET(P, eng, ap, val, w):
    P.op(eng, lambda e: e.memset(ap, val), r=(), w=w)


class Ctx:
    pass


def build_program(seq_lens, depth=DEPTH, debug=False, stop_after=None):
    nc = bass.Bass("TRN2", target_bir_lowering=False)
    P = Prog(nc)
    C = Ctx()
    C.P, C.nc, C.depth = P, nc, depth
    Smax = max(seq_lens)
    C.Smax = Smax
    skind = "ExternalOutput" if debug else "Internal"

    def dram(name, shape, dt, kind):
        return nc.dram_tensor(name, list(shape), dt, kind=kind).ap()

    C.xin = [dram(f"x{i}", [8, 128, S], F32, "ExternalInput") for i, S in enumerate(seq_lens)]
    C.yout = [dram(f"y{i}", [8, 128, S], F32, "ExternalOutput") for i, S in enumerate(seq_lens)]
    C.w8 = dram("w8", [depth, NCH8, 128, 1024], F32, "ExternalInput")
    C.w22 = dram("w22", [depth, 16, 128, DFF], F32, "ExternalInput")
    C.wv = dram("wv", [depth, 128, 8 * 512], F32, "ExternalInput")
    C.wgate = dram("wgate", [depth, 2, 2, 2, 128, 128], F32, "ExternalInput")
    C.small = dram("small", [128, NSMALL], F32, "ExternalInput")
    C.c64 = dram("c64", [128, 2 * 512], F32, "ExternalInput")
    C.kext = dram("kext", [4, 4, Smax], F32, "ExternalInput")
    C.qext = dram("qext", [2, 4, 4, Smax], F32, "ExternalInput")
    C.dbig = dram("dbig", [128, 896], F32, "ExternalInput")
    C.kio = dram("kio", [2, Smax], F32, "ExternalInput")
    C.tcol = dram("tcol", [128, 2 * 2 * 64], F32, "ExternalInput")
    C.w8b = dram("w8b", [depth, NCH8, 128, 1024], BF16, "Internal")
    C.w22b = dram("w22b", [depth, 16, 128, DFF], BF16, "Internal")
    C.x1T = dram("x1T", [8, 128, Smax], F32, skind)
    C.rxg = dram("rxg", [4, 128, Smax], F32, skind)
    C.qk = dram("qk", [8, 128, Smax], BF16, skind)
    C.G = dram("G", [Smax, 512], BF16, skind)
    C.v = dram("v", [Smax, 512], BF16, skind)
    C.yT = dram("yT", [8, 128, Smax], BF16, skind)
    C.dft = {}
    for S in sorted(set(seq_lens)):
        C.dft[S] = dram(f"dft{S}", [2, S // 128, 128, S], BF16, "Internal")

    C.ps2 = [nc.alloc_psum_tensor(f"psp{i}", [128, 1024], F32) for i in range(4)]
    C.ps = [PBuf(P, C.ps2[i // 2], i % 2) for i in range(8)]
    for b_ in C.ps:
        b_.excl = True

    C.smallt = P.sb("small", [128, NSMALL], F32)
    C.ones_ln = P.sb("ones_ln", [128, 128], BF16)
    C.ones1 = P.sb("ones1", [128, 128], BF16)
    C.ones_sub = P.sb("ones_sub", [128, 128], BF16)
    DMA(P, C.smallt[:], C.small[:, :], w=[C.smallt])
    MSET(P, "dve", C.ones_ln[:], 1.0 / 1024.0, [C.ones_ln])
    MSET(P, "dve", C.ones1[:], 1.0, [C.ones1])
    MSET(P, "dve", C.ones_sub[:], 1.0 / 128.0, [C.ones_sub])
    C.base_off = P.sb_off
    bg_alloc(C, min(2048, min(seq_lens)))

    setup_weights(C)
    P.barrier()
    for vi, S_ in enumerate(sorted(set(seq_lens))):
        for _ in dft_gen(C, S_, vi):
            pass
    P.barrier()
    if stop_after == "setup":
        P.emit()
        return nc
    for si, S in enumerate(seq_lens):
        for k in range(depth + 1):
            tl_phase(C, si, S, k)
            if k < depth:
                bg_drain(C, S)
            P.barrier()
            if stop_after == ("tl", si, k):
                P.emit()
                return nc
            if k < depth:
                mixer_phase(C, si, S, k)
                P.barrier()
    P.emit()
    return nc


def setup_weights(C):
    P = C.P
    for l in range(C.depth):
        for c0 in range(0, NCH8, 10):
            c1 = min(NCH8, c0 + 10)
            DMA(P, C.w8b[l, c0:c1], C.w8[l, c0:c1], eng="pool")
        for c0 in range(0, 16, 4):
            DMA(P, C.w22b[l, c0:c0 + 4], C.w22[l, c0:c0 + 4], eng="pool")


def sm(C, col, n=1):
    return C.smallt[:, col:col + n]


class CBuf:
    def __init__(self, P, name, shape, dtype):
        b = P.sb(name, shape, dtype)
        self.t = b.t
        self.c = [b] + [P.wrap(b.t) for _ in range(shape[1] - 1)]

    def __getitem__(self, k):
        return self.t[k]


def tl_phase(C, si, S, k):
    P, nc, depth = C.P, C.nc, C.depth
    P.sb_off = C.base_off
    has_out = k > 0
    has_in = k < depth
    ps = C.ps
    xfs = [CBuf(P, f"xf{i}", [128, 8, T], F32) for i in range(2)]
    xb = CBuf(P, "xb", [128, 8, T], BF16)
    sq = CBuf(P, "sq", [128, 8, T], BF16)
    hact = CBuf(P, "hact", [128, NF, T], BF16)
    sg = [P.sb(f"sg{i}", [128, T], F32) for i in range(2)]
    tmp = [P.sb(f"tmp{i}", [128, T], F32) for i in range(3)]
    mean = P.sb("mean", [128, T], F32)
    msq = P.sb("msq", [128, T], F32)
    rstd = P.sb("rstd", [128, T], F32)
    mr = P.sb("mr", [128, T], F32)
    w8r = [P.sb(f"w8r{i}", [128, 8, 128], BF16) for i in range(7)]
    w22r = [P.sb(f"w22r{i}", [128, NF, 128], BF16) for i in range(4)]
    if has_out:
        ybs = [CBuf(P, f"yb{i}", [128, 8, T], BF16) for i in range(2)]
    if has_in:
        wvt = P.sb("wvt", [128, 8, 512], BF16)
        c64t = P.sb("c64t", [128, 2, 512], BF16)
        rxg_st = P.sb("rxg_st", [128, 4, T], F32)
        qk_st = P.sb("qk_st", [128, 8, T], BF16)
        fxT = P.sb("fxT", [128, 2, T], BF16)
        g_st = P.sb("g_st", [128, 4, 512], BF16)
        v_st = P.sb("v_st", [128, 4, 512], BF16)
        DMA(P, wvt[:].rearrange("p a b -> p (a b)"), C.wv[k], w=[wvt], eng="pool")
        DMA(P, c64t[:].rearrange("p a b -> p (a b)"), C.c64[:, :], w=[c64t], eng="pool")

    w8_state = {"i": 0}
    w22_state = {"i": 0}

    def ws_linear(l, chunks, rhs, epilogue, ps_ids, group_first=False):
        n = len(chunks)
        Dp = 4
        slots = {}
        nxt = {"i": 0}

        def load_upto(m):
            while nxt["i"] < min(m, n):
                i = nxt["i"]
                nxt["i"] += 1
                s_ = w8_state["i"] % len(w8r)
                w8_state["i"] += 1
                slots[i] = w8r[s_]
                DMA(P, w8r[s_][:].rearrange("p a b -> p (a b)"), C.w8b[l, chunks[i]], w=[w8r[s_]])
        load_upto(Dp)
        i = 0
        if group_first and n >= 4:
            load_upto(6)
            for kc in range(8):
                for g in range(4):
                    pb = ps[ps_ids[g % len(ps_ids)]]
                    MM(P, pb, pb[:], slots[g][:, kc, :], rhs[:, kc, :], kc == 0, kc == 7, [slots[g], rhs.c[kc]])
            for g in range(4):
                slots.pop(g)
                epilogue(g, ps[ps_ids[g % len(ps_ids)]])
            i = 4
        while i < n:
            load_upto(i + Dp + 1)
            wt = slots.pop(i)
            pb = ps[ps_ids[i % len(ps_ids)]]
            for kc in range(8):
                MM(P, pb, pb[:], wt[:, kc, :], rhs[:, kc, :], kc == 0, kc == 7, [wt, rhs.c[kc]])
            epilogue(i, pb)
            bg_step(C)
            i += 1

    def stats_prep(c):
        xf = cur["xf"]
        CP(P, "dve", xb[:, c, :], xf[:, c, :], [xf.c[c]], [xb.c[c]])
        ACTV(P, sq[:, c, :], xf[:, c, :], AF.Square, [xf.c[c]], [sq.c[c]])

    def stats_mm(c):
        MM(P, ps[6], ps[6][:], C.ones_ln[:], xb[:, c, :], c == 0, c == 7, [C.ones_ln, xb.c[c]])
        MM(P, ps[7], ps[7][:], C.ones_ln[:], sq[:, c, :], c == 0, c == 7, [C.ones_ln, sq.c[c]])

    def ffn(l, cbase, w22base, ln_idx):
        xf = cur["xf"]
        def ep(i, pb):
            j = i // 2
            if i % 2 == 0:
                ACTV(P, sg[j % 2][:], pb[:], AF.Silu, [pb], [sg[j % 2]])
            else:
                TT(P, "dve", hact[:, j, :], sg[j % 2][:], pb[:], ALU.mult, [sg[j % 2], pb], [hact.c[j]])
        ws_linear(l, [cbase + i for i in range(2 * NF)], xb, ep, [0, 1, 2, 3], group_first=True)
        slots = {}

        def load(c):
            s_ = w22_state["i"] % len(w22r)
            w22_state["i"] += 1
            slots[c] = w22r[s_]
            DMA(P, w22r[s_][:].rearrange("p a b -> p (a b)"), C.w22b[l, w22base + c], w=[w22r[s_]])
        load(0)
        load(1)
        for c in range(8):
            if c + 2 < 8:
                load(c + 2)
            wt = slots.pop(c)
            pb = ps[4 + c % 2]
            for kc in range(NF):
                MM(P, pb, pb[:], wt[:, kc, :], hact[:, kc, :], kc == 0, kc == NF - 1, [wt, hact.c[kc]])
            STT(P, "dve", xf[:, c, :], pb[:], 0.5 / ALPHA, xf[:, c, :], ALU.mult, ALU.add, [pb, xf.c[c]], [xf.c[c]])
            stats_prep(c)
            if c > 0:
                stats_mm(c - 1)
            bg_step(C)
        stats_mm(7)
        layer_norm(l, ln_idx)

    def layer_norm(l, idx):
        eps = LN_EPS / (ALPHA * ALPHA)
        xf = cur["xf"]
        CP(P, "dve", mean[:], ps[6][:], [ps[6]], [mean])
        TT(P, "dve", msq[:], mean[:], mean[:], ALU.mult, [mean], [msq])
        STT(P, "dve", rstd[:], ps[7][:], eps, msq[:], ALU.add, ALU.subtract, [ps[7], msq], [rstd])
        ACTV(P, rstd[:], rstd[:], AF.Ln, [rstd], [rstd])
        ACTV(P, rstd[:], rstd[:], AF.Exp, [rstd], [rstd], scale=-0.5)
        gcol = SM_LNG + (l * 3 + idx) * 8
        bcol = SM_LNB + (l * 3 + idx) * 8
        for c in range(8):
            tb = tmp[c % 3]
            TT(P, "dve", tb[:], xf[:, c, :], mean[:], ALU.subtract, [xf.c[c], mean], [tb])
            TT(P, "dve", tb[:], tb[:], rstd[:], ALU.mult, [tb, rstd], [tb])
            ACTV(P, xb[:, c, :], tb[:], AF.Identity, [tb, C.smallt], [xb.c[c]],
                 bias=sm(C, bcol + c), scale=sm(C, gcol + c))
            ACTV(P, xf[:, c, :], tb[:], AF.Identity, [tb, C.smallt], [xf.c[c]],
                 bias=sm(C, bcol + c), scale=sm(C, gcol + c))

    nt = S // T
    IOQ = "pool"
    cur = {}

    def load_tile(ti):
        tsl_ = slice(ti * T, (ti + 1) * T)
        xf_ = xfs[ti % 2]
        if k == 0:
            DMA(P, xf_[:], C.xin[si][:, :, tsl_].rearrange("c p t -> p c t"), w=xf_.c, eng=IOQ)
        else:
            DMA(P, xf_[:], C.x1T[:, :, tsl_].rearrange("c p t -> p c t"), w=xf_.c, eng=IOQ)
            DMA(P, ybs[ti % 2][:], C.yT[:, :, tsl_].rearrange("c p t -> p c t"), w=ybs[ti % 2].c, eng=IOQ)
    load_tile(0)
    for ti in range(nt):
        tsl = slice(ti * T, (ti + 1) * T)
        if ti + 1 < nt:
            load_tile(ti + 1)
        xf = xfs[ti % 2]
        cur["xf"] = xf
        if has_out:
            yb = ybs[ti % 2]
        if k == 0:
            for c in range(8):
                CP(P, "dve" if c % 2 else "act", xb[:, c, :], xf[:, c, :], [xf.c[c]], [xb.c[c]])
        if has_out:
            lo = k - 1

            def ep_o(i, pb):
                STT(P, "dve", xf[:, i, :], pb[:], 1.0 / ALPHA, xf[:, i, :], ALU.mult, ALU.add, [pb, xf.c[i]], [xf.c[i]])
                stats_prep(i)
                if i > 0:
                    stats_mm(i - 1)
            ws_linear(lo, [C_OUT + i for i in range(8)], yb, ep_o, [0, 1, 2, 3])
            stats_mm(7)
            layer_norm(lo, 1)
            ffn(lo, C_GU2, 8, 2)
        if not has_in:
            DMA(P, C.yout[si][:, :, tsl].rearrange("c p t -> p c t"), xf[:], r=xf.c, eng=IOQ)
            continue
        ffn(k, C_GU1, 0, 0)
        DMA(P, C.x1T[:, :, tsl].rearrange("c p t -> p c t"), xf[:], r=xf.c, eng=IOQ)

        def ep_i(i, pb):
            if i < 4:
                CP(P, "act", rxg_st[:, i, :], pb[:], [pb], [rxg_st])
            elif i < 12:
                CP(P, "dve" if i % 2 else "act", qk_st[:, i - 4, :], pb[:], [pb], [qk_st])
            else:
                CP(P, "dve", fxT[:, i - 12, :], pb[:], [pb], [fxT])
        ws_linear(k, [C_IN + i for i in range(14)], xb, ep_i, [0, 1, 2, 3], group_first=True)
        DMA(P, C.rxg[:, :, tsl].rearrange("c p t -> p c t"), rxg_st[:], r=[rxg_st], eng=IOQ)
        DMA(P, C.qk[:, :, tsl].rearrange("c p t -> p c t"), qk_st[:], r=[qk_st], eng=IOQ)
        for tb in range(4):
            pb = ps[4 + tb % 2]
            for j in range(2):
                MM(P, pb, pb[:], fxT[:, j, tb * 128:(tb + 1) * 128], c64t[:, j, :], j == 0, j == 1, [fxT, c64t])
            CP(P, "act", g_st[:, tb, :], pb[:], [pb], [g_st])
        DMA(P, C.G[tsl, :].rearrange("(b p) f -> p b f", p=128), g_st[:], r=[g_st], eng=IOQ)
        for tb in range(4):
            pb = ps[tb % 4]
            for kc in range(8):
                MM(P, pb, pb[:], xb[:, kc, tb * 128:(tb + 1) * 128], wvt[:, kc, :], kc == 0, kc == 7, [xb.c[kc], wvt])
            CP(P, "dve", v_st[:, tb, :], pb[:], [pb], [v_st])
        DMA(P, C.v[tsl, :].rearrange("(b p) f -> p b f", p=128), v_st[:], r=[v_st], eng=IOQ)


def bg_alloc(C, W):
    P = C.P
    B = Ctx()
    B.W = W
    B.klo = P.sb("bg_klo", [128, W], F32)
    B.khi = P.sb("bg_khi", [128, W], F32)
    B.tct = P.sb("bg_tct", [128, 256], F32)
    B.c1j = P.sb("bg_c1j", [128, 16, 64], F32)
    B.base = P.sb("bg_base", [128, W], F32)
    B.pp = [P.sb(f"bg_pp{i}", [128, W], F32) for i in range(2)]
    B.rn = [P.sb(f"bg_rn{i}", [128, W], F32) for i in range(2)]
    B.ab = [P.sb(f"bg_ab{i}", [128, W], F32) for i in range(2)]
    B.oc = [P.sb(f"bg_oc{i}", [128, W], BF16) for i in range(2)]
    B.os = [P.sb(f"bg_os{i}", [128, W], BF16) for i in range(2)]
    DMA(P, B.klo[:], C.kio[0, 0:W].partition_broadcast(128), w=[B.klo])
    DMA(P, B.khi[:], C.kio[1, 0:W].partition_broadcast(128), w=[B.khi])
    DMA(P, B.tct[:], C.tcol[:, :], w=[B.tct])
    C.B = B
    C.bg = []
    C.bg_calls = 0


def dft_gen(C, S, vi):
    P, B = C.P, C.B
    W = B.W
    A = S // 128
    nj = S // W
    for j in range(nj):
        TS(P, "dve", B.c1j[:, j, :], B.tct[:, vi * 128: vi * 128 + 64], (W // 64) * 1.0 * j, None, ALU.mult, None,
           [B.tct], [B.c1j])
    it = 0
    for a in range(A):
        c1 = B.tct[:, vi * 128 + a: vi * 128 + a + 1]
        c2 = B.tct[:, vi * 128 + 64 + a: vi * 128 + 64 + a + 1]
        TS(P, "dve", B.base[:], B.klo[:], c2, None, ALU.mult, None, [B.klo, B.tct], [B.base])
        STT(P, "dve", B.base[:], B.khi[:], c1, B.base[:], ALU.mult, ALU.add, [B.khi, B.tct, B.base], [B.base])
        for j in range(nj):
            b = it % 2
            it += 1
            sl = slice(j * W, (j + 1) * W)
            pp, rn, ab, oc, os_ = B.pp[b], B.rn[b], B.ab[b], B.oc[b], B.os[b]
            TS(P, "dve", pp[:], B.base[:], B.c1j[:, j, a:a + 1], None, ALU.add, None, [B.base, B.c1j], [pp])
            TS(P, "dve", rn[:], pp[:], MAGIC, MAGIC, ALU.add, ALU.subtract, [pp], [rn])
            TT(P, "dve", pp[:], pp[:], rn[:], ALU.subtract, [pp, rn], [pp])
            ACTV(P, os_[:], pp[:], AF.Sin, [pp], [os_], scale=-6.28318)
            ACTV(P, ab[:], pp[:], AF.Abs, [pp], [ab])
            ACTV(P, oc[:], ab[:], AF.Sin, [ab], [oc], scale=6.28318, bias=-1.570796)
            DMA(P, C.dft[S][0, a, :, sl], oc[:], r=[oc])
            DMA(P, C.dft[S][1, a, :, sl], os_[:], r=[os_])
            yield


def bg_step(C):
    if not C.bg:
        return
    C.bg_calls += 1
    S_, g, every = C.bg[0]
    if C.bg_calls % every:
        return
    try:
        for _ in range(BG_BATCH):
            next(g)
    except StopIteration:
        C.bg.pop(0)


def bg_drain(C, S):
    while any(e[0] == S for e in C.bg):
        S_, g, every = C.bg[0]
        for _ in g:
            pass
        C.bg.pop(0)


def run_interleaved(gens):
    gens = list(gens)
    while gens:
        for g in list(gens):
            try:
                next(g)
            except StopIteration:
                gens.remove(g)


def mixer_phase(C, si, S, l):
    C.P.sb_off = C.base_off
    run_interleaved([rg_phase(C, S, l), fnet_phase(C, S, l)])
    C.P.barrier()
    attn_phase(C, S, l)


def rg_phase(C, S, l):
    P, ps = C.P, C.ps
    TT_ = min(S, 2048) if S <= 4096 else 1024
    ntt = S // TT_
    sets = []
    for i in range(2):
        sets.append(dict(
            rxp=P.sb(f"rxp{i}", [128, TT_ + 3], F32), xc=P.sb(f"xc{i}", [128, TT_], F32),
            xcb=P.sb(f"xcb{i}", [128, TT_], BF16), A=P.sb(f"rgA{i}", [128, TT_], F32),
            U=P.sb(f"rgU{i}", [128, TT_], F32), G1=P.sb(f"rgG{i}", [128, TT_], F32),
            H2=P.sb(f"rgH2{i}", [128, TT_], F32), ob=P.sb(f"rgob{i}", [128, TT_], BF16)))
    Hf = P.sb("rgHf", [128, S], F32)
    step = 0
    wgt = [[P.sb(f"wg{d}{a}", [128, 128], BF16) for a in range(2)] for d in range(2)]
    kap = P.sb("kap", [128, 4], F32)
    carry = P.sb("carry", [128, 1], F32)
    for c in range(2):
        for d in range(2):
            for a in range(2):
                DMA(P, wgt[d][a][:], C.wgate[l, d, a, c], w=[wgt[d][a]], eng="pool")
            lamc = sm(C, SM_LAM + (l * 2 + d) * 2 + c)
            ACTV(P, kap[:, d:d + 1], lamc, AF.Exp, [C.smallt], [kap], scale=-1.0)
            ACTV(P, kap[:, d:d + 1], kap[:, d:d + 1], AF.Ln, [kap], [kap], bias=1.0)
            TS(P, "dve", kap[:, d:d + 1], kap[:, d:d + 1], -RG_C, None, ALU.mult, None, [kap], [kap])
        for d in range(2):
            order = list(range(ntt)) if d == 0 else list(range(ntt - 1, -1, -1))
            for oi, tt in enumerate(order):
                B_ = sets[step % 2]
                step += 1
                rxp, xc, xcb, A, U, G1, H2, ob = (B_[k_] for k_ in ("rxp", "xc", "xcb", "A", "U", "G1", "H2", "ob"))
                t0 = tt * TT_
                lo, hi = t0 - 2, t0 + TT_ + 1
                dlo, slo = 0, lo
                if lo < 0:
                    MSET(P, "dve", rxp[:, 0:2], 0.0, [rxp])
                    dlo, slo = 2, 0
                shi = hi
                if hi > S:
                    MSET(P, "dve", rxp[:, TT_ + 2:TT_ + 3], 0.0, [rxp])
                    shi = S
                DMA(P, rxp[:, dlo:dlo + (shi - slo)], C.rxg[c, :, slo:shi], w=[rxp])
                TS(P, "dve", xc[:], rxp[:, 0:TT_], sm(C, SM_CONVW + (l * 4 + 0) * 2 + c), sm(C, SM_CONVB + l * 2 + c),
                   ALU.mult, ALU.add, [rxp, C.smallt], [xc])
                for j in range(1, 4):
                    STT(P, "dve", xc[:], rxp[:, j:j + TT_], sm(C, SM_CONVW + (l * 4 + j) * 2 + c), xc[:],
                        ALU.mult, ALU.add, [rxp, xc, C.smallt], [xc])
                CP(P, "act", xcb[:], xc[:], [xc], [xcb])
                yield
                bac = sm(C, SM_BA + (l * 2 + d) * 2 + c)
                bxc = sm(C, SM_BX + (l * 2 + d) * 2 + c)
                for b in range(TT_ // 512):
                    bs = slice(b * 512, (b + 1) * 512)
                    pr, pi = ps[(b % 2) * 2], ps[(b % 2) * 2 + 1]
                    MM(P, pr, pr[:], wgt[d][0][:], xcb[:, bs], True, True, [wgt[d][0], xcb])
                    MM(P, pi, pi[:], wgt[d][1][:], xcb[:, bs], True, True, [wgt[d][1], xcb])
                    ACTV(P, A[:, bs], pr[:], AF.Sigmoid, [pr, C.smallt], [A], bias=bac)
                    ACTV(P, G1[:, bs], pi[:], AF.Sigmoid, [pi, C.smallt], [G1], bias=bxc)
                    yield
                ACTV(P, A[:], A[:], AF.Exp, [A, kap], [A], scale=kap[:, d:d + 1])
                ACTV(P, U[:], A[:], AF.Square, [A], [U])
                ACTV(P, U[:], U[:], AF.Sqrt, [U], [U], scale=-1.0, bias=1.0)
                TT(P, "dve", U[:], U[:], G1[:], ALU.mult, [U, G1], [U])
                TT(P, "dve", U[:], U[:], xc[:], ALU.mult, [U, xc], [U])
                yield
                CH = min(TT_, 2048)
                nch = TT_ // CH
                if d == 0:
                    for q in range(nch):
                        g0 = t0 + q * CH
                        init = 0.0 if g0 == 0 else Hf[:, g0 - 1:g0]
                        P.op("dve", lambda e, o=Hf[:, g0:g0 + CH], a_=A[:, q * CH:(q + 1) * CH], u_=U[:, q * CH:(q + 1) * CH], i_=init:
                             e.tensor_tensor_scan(out=o, data0=a_, data1=u_, initial=i_, op0=ALU.mult, op1=ALU.add),
                             r=[A, U, Hf], w=[Hf])
                else:
                    for q in range(nch - 1, -1, -1):
                        first = (oi == 0 and q == nch - 1)
                        init = 0.0 if first else carry[:, 0:1]
                        sl = slice(q * CH, (q + 1) * CH)
                        P.op("dve", lambda e, o=H2[:, sl][:, ::-1], a_=A[:, sl][:, ::-1], u_=U[:, sl][:, ::-1], i_=init:
                             e.tensor_tensor_scan(out=o, data0=a_, data1=u_, initial=i_, op0=ALU.mult, op1=ALU.add),
                             r=[A, U, carry], w=[H2])
                        CP(P, "dve", carry[:, 0:1], H2[:, q * CH:q * CH + 1], [H2], [carry])
                    yield
                    DMA(P, rxp[:, 0:TT_], C.rxg[2 + c, :, t0:t0 + TT_], w=[rxp])
                    rg = rxp[:, 0:TT_]
                    ACTV(P, G1[:], rg, AF.Square, [rxp], [G1])
                    TS(P, "dve", G1[:], G1[:], 0.044715, 1.0, ALU.mult, ALU.add, [G1], [G1])
                    TT(P, "dve", G1[:], G1[:], rg, ALU.mult, [G1, rxp], [G1])
                    ACTV(P, G1[:], G1[:], AF.Sigmoid, [G1], [G1], scale=1.5957691216)
                    TT(P, "dve", G1[:], G1[:], rg, ALU.mult, [G1, rxp], [G1])
                    TT(P, "dve", H2[:], H2[:], Hf[:, t0:t0 + TT_], ALU.add, [H2, Hf], [H2])
                    TT(P, "dve", ob[:], H2[:], G1[:], ALU.mult, [H2, G1], [ob])
                    DMA(P, C.yT[c, :, t0:t0 + TT_], ob[:], r=[ob])
                yield


def attn_phase(C, S, l):
    P, ps = C.P, C.ps
    P.sb_off = C.base_off
    nkb = S // 128
    nqt = S // 512
    lam_init = 0.8 - 0.6 * math.exp(-0.3 * l)
    kxs = [[P.sb(f"kx{i}{m}", [128, S], BF16) for m in range(2)] for i in range(2)]
    vts = [P.sb(f"vt{i}", [128, nkb, 128], BF16) for i in range(2)]
    qa = [[P.sb(f"qa{i}{m}", [128, 512], BF16) for m in range(2)] for i in range(2)]
    qb = [[P.sb(f"qb{i}{m}", [128, 512], BF16) for m in range(2)] for i in range(2)]
    dbt = P.sb("dbt", [128, 896], F32)
    pT = [P.sb(f"pT{i}", [128, 2, 512], BF16) for i in range(8)]
    sbias = [P.sb(f"sbias{i}", [128, 2, 512], F32) for i in range(4)]
    l1s = P.sb("al1s", [128, 512], F32)
    accL = [P.sb(f"accL{i}", [128, 2, 512], F32) for i in range(2)]
    osb = [P.sb(f"osb{m}", [128, 512], F32) for m in range(2)]
    r0 = P.sb("ar0", [128, 512], F32)
    t0_ = P.sb("at0", [128, 512], F32)
    t1_ = P.sb("at1", [128, 512], F32)
    oo = P.sb("aoo", [128, 512], F32)
    osq = P.sb("aosq", [128, 512], BF16)
    rs = P.sb("ars", [128, 512], F32)
    obuf = [P.sb(f"aob{i}", [128, 512], BF16) for i in range(2)]
    lq = P.sb("lq", [128, 128], F32)
    sc = P.sb("asc", [128, 8], F32)
    ones_f = P.sb("ones_f", [128, 128], F32)
    MSET(P, "dve", ones_f[:], 1.0, [ones_f])
    DMA(P, dbt[:], C.dbig[:, :], w=[dbt])
    base = SM_LQK + l * 256
    TT(P, "dve", lq[:, 0:64], sm(C, base, 64), sm(C, base + 64, 64), ALU.mult, [C.smallt], [lq])
    TT(P, "dve", lq[:, 64:128], sm(C, base + 128, 64), sm(C, base + 192, 64), ALU.mult, [C.smallt], [lq])
    P.op("dve", lambda e: e.reduce_sum(out=sc[:, 0:1], in_=lq[:, 0:64], axis=mybir.AxisListType.X), r=[lq], w=[sc])
    P.op("dve", lambda e: e.reduce_sum(out=sc[:, 1:2], in_=lq[:, 64:128], axis=mybir.AxisListType.X), r=[lq], w=[sc])
    ACTV(P, sc[:, 2:4], sc[:, 0:2], AF.Exp, [sc], [sc])
    TT(P, "dve", sc[:, 4:5], sc[:, 3:4], sc[:, 2:3], ALU.subtract, [sc], [sc])
    TS(P, "dve", sc[:, 4:5], sc[:, 4:5], -lam_init, None, ALU.add, None, [sc], [sc])
    TS(P, "dve", sc[:, 5:6], sm(C, SM_SUBG + l), 1.0 - lam_init, None, ALU.mult, None, [C.smallt], [sc])
    neglam = sc[:, 4:5]
    gsc = sc[:, 5:6]
    it = 0
    pend_tail = []
    drate = 3 if nkb < 16 else 1
    def load_kv(h_):
        kx_, vt_ = kxs[h_ % 2], vts[h_ % 2]
        for m in range(2):
            DMA(P, kx_[m][0:64, :], C.qk[4 + h_, m * 64:(m + 1) * 64, 0:S], w=[kx_[m]])
            DMA(P, kx_[m][64:68, :], C.kext[h_, :, 0:S], w=[kx_[m]], eng="pool")
        DMA(P, vt_[:], C.v[0:S, h_ * 128:(h_ + 1) * 128].rearrange("(b p) e -> p b e", p=128), w=[vt_])
    load_kv(0)
    for h in range(4):
        s8 = 8.0 * SLOPES[h]
        kx, vt = kxs[h % 2], vts[h % 2]
        if h + 1 < 4:
            load_kv(h + 1)

        def load_q(qt):
            qs = slice(qt * 512, qt * 512 + 512)
            qi = qt % 2
            for m in range(2):
                DMA(P, qa[qi][m][0:64, :], C.qk[h, m * 64:(m + 1) * 64, qs], w=[qa[qi][m]])
                DMA(P, qa[qi][m][64:68, :], C.qext[0, h, :, qs], w=[qa[qi][m]], eng="pool")
                DMA(P, qb[qi][m][0:64, :], C.qk[h, m * 64:(m + 1) * 64, qs], w=[qb[qi][m]])
                DMA(P, qb[qi][m][64:68, :], C.qext[1, h, :, qs], w=[qb[qi][m]], eng="pool")
        load_q(0)
        for qt in range(nqt):
            Q0 = qt * 512
            qs = slice(Q0, Q0 + 512)
            qi = qt % 2
            if qt + 1 < nqt:
                load_q(qt + 1)
            al = accL[qt % 2]

            def qk_mm(kb, itn):
                K0 = kb * 128
                ks = slice(K0, K0 + 128)
                srcs = []
                for m in range(2):
                    sp = ps[(itn % 2) * 2 + m]
                    if K0 + 128 <= Q0:
                        MM(P, sp, sp[:], kx[m][0:68, ks], qa[qi][m][0:68, :], True, True, [kx[m], qa[qi][m]])
                        srcs.append((sp[:], sp, None))
                    elif K0 >= Q0 + 512:
                        MM(P, sp, sp[:], kx[m][0:68, ks], qb[qi][m][0:68, :], True, True, [kx[m], qb[qi][m]])
                        srcs.append((sp[:], sp, None))
                    else:
                        j = (K0 - Q0) // 128
                        MM(P, sp, sp[:], kx[m][0:64, ks], qa[qi][m][0:64, :], True, True, [kx[m], qa[qi][m]])
                        sbb = sbias[itn % 4]
                        STT(P, "dve", sbb[:, m, :], dbt[:, 384 - 128 * j: 384 - 128 * j + 512], -s8, sp[:],
                            ALU.mult, ALU.add, [dbt, sp], [sbb])
                        srcs.append((sp[:], sp, j))
                return srcs
            pend = {0: qk_mm(0, it)}
            if nkb > 1:
                pend[1] = qk_mm(1, it + 1)
            for kb in range(nkb):
                cur = pend.pop(kb)
                itc = it
                pslot = pT[it % 8]
                if cur[0][2] is not None:
                    sbb = sbias[it % 4]
                    ACTV(P, pslot[:].rearrange("p a b -> p (a b)"), sbb[:].rearrange("p a b -> p (a b)"), AF.Exp,
                         [sbb], [pslot], scale=0.125)
                else:
                    pair = C.ps2[itc % 2]
                    ACTV(P, pslot[:].rearrange("p a b -> p (a b)"), pair[:, :], AF.Exp,
                         [cur[0][1], cur[1][1]], [pslot], scale=0.125)
                if kb + 2 < nkb:
                    pend[kb + 2] = qk_mm(kb + 2, it + 2)
                if nkb < 32 or kb % 2 == 1:
                    for _ in range(drate):
                        if pend_tail:
                            pend_tail.pop(0)()
                for m in range(2):
                    MM(P, ps[4 + m], ps[4 + m][:], vt[:, kb, :], pslot[:, m, :], kb == 0, kb == nkb - 1, [vt, pslot])
                MM(P, ps[6], ps[6][:], C.ones1[:], pslot[:, 1, :], kb == 0, kb == nkb - 1, [C.ones1, pslot])
                if kb == 0:
                    CP(P, "dve", al[:, 0, :], pslot[:, 0, :], [pslot], [al])
                else:
                    TT(P, "dve", al[:, 0, :], al[:, 0, :], pslot[:, 0, :], ALU.add, [al, pslot], [al])
                it += 1
            while pend_tail:
                pend_tail.pop(0)()
            for m in range(2):
                CP(P, "dve", osb[m][:], ps[4 + m][:], [ps[4 + m]], [osb[m]])
            CP(P, "dve", l1s[:], ps[6][:], [ps[6]], [l1s])
            MM(P, ps[7], ps[7][:], ones_f[:], al[:, 0, :], True, True, [ones_f, al])
            ob_ = obuf[qt % 2]
            pend_tail.extend([
                lambda: RECIP(P, r0[:], ps[7][:], [ps[7]], [r0]),
                lambda: TT(P, "dve", t0_[:], osb[0][:], r0[:], ALU.mult, [osb[0], r0], [t0_]),
                lambda: RECIP(P, r0[:], l1s[:], [l1s], [r0]),
                lambda: TT(P, "dve", t1_[:], osb[1][:], r0[:], ALU.mult, [osb[1], r0], [t1_]),
                lambda: STT(P, "dve", oo[:], t1_[:], neglam, t0_[:], ALU.mult, ALU.add, [t1_, t0_, sc], [oo]),
                lambda: TT(P, "dve", osq[:], oo[:], oo[:], ALU.mult, [oo], [osq]),
                lambda: MM(P, ps[7], ps[7][:], C.ones_sub[:], osq[:], True, True, [C.ones_sub, osq]),
                lambda: TS(P, "dve", rs[:], ps[7][:], NORM_EPS, None, ALU.add, None, [ps[7]], [rs]),
                lambda: ACTV(P, rs[:], rs[:], AF.Ln, [rs], [rs]),
                lambda: ACTV(P, rs[:], rs[:], AF.Exp, [rs], [rs], scale=-0.5),
                lambda ob_=ob_: STT(P, "dve", ob_[:], oo[:], gsc, rs[:], ALU.mult, ALU.mult, [oo, rs, sc], [ob_]),
                lambda ob_=ob_, h=h, qs=qs: DMA(P, C.yT[2 + h, :, qs], ob_[:], r=[ob_]),
            ])
    while pend_tail:
        pend_tail.pop(0)()


def fnet_phase(C, S, l):
    P, ps = C.P, C.ps
    nb = S // 128
    PC = 4
    Gt = P.sb("Gt", [128, nb, 512], BF16)
    cr = [P.sb(f"fcr{i}", [128, PC, 512], BF16) for i in range(2)]
    sr = [P.sb(f"fsr{i}", [128, PC, 512], BF16) for i in range(2)]
    obuf = [P.sb(f"fob{i}", [128, 512], BF16) for i in range(4)]
    DMA(P, Gt[:], C.G[0:S, :].rearrange("(b p) f -> p b f", p=128), w=[Gt])
    norm = 1.0 / math.sqrt(S * 64.0)
    it = 0
    for kt in range(S // 512):
        ksl = slice(kt * 512, (kt + 1) * 512)
        for pc in range(nb // PC):
            ct, st_ = cr[it % 2], sr[it % 2]
            it += 1
            DMA(P, ct[:], C.dft[S][0, pc * PC:(pc + 1) * PC, :, ksl].rearrange("a p k -> p a k"), w=[ct])
            DMA(P, st_[:], C.dft[S][1, pc * PC:(pc + 1) * PC, :, ksl].rearrange("a p k -> p a k"), w=[st_])
            for a in range(PC):
                tb = pc * PC + a
                for fc in range(2):
                    pb = ps[4 + (kt % 2) * 2 + fc]
                    MM(P, pb, pb[:], Gt[:, tb, fc * 128:(fc + 1) * 128], ct[:, a, :], tb == 0, False, [Gt, ct])
                    MM(P, pb, pb[:], Gt[:, tb, 256 + fc * 128:256 + (fc + 1) * 128], st_[:, a, :], False, tb == nb - 1, [Gt, st_])
            yield
        for fc in range(2):
            pb = ps[4 + (kt % 2) * 2 + fc]
            ob = obuf[(kt % 2) * 2 + fc]
            ACTV(P, ob[:], pb[:], AF.Identity, [pb], [ob], scale=norm)
            DMA(P, C.yT[6 + fc, :, ksl], ob[:], r=[ob])
        yield


def _ws(W, n0, kc):
    sub = W[:, n0:n0 + 128].reshape(kc, 128, 128)
    return np.ascontiguousarray(sub.transpose(1, 0, 2)).reshape(128, kc * 128)


def _col(v):
    return np.ascontiguousarray(v.reshape(-1, 128).T)


def prep_shared(inp, depth, Smax, svars):
    f32 = np.float32
    w8 = np.zeros([depth, NCH8, 128, 1024], f32)
    w22 = np.zeros([depth, 16, 128, DFF], f32)
    wv = np.zeros([depth, 128, 8 * 512], f32)
    wgate = np.zeros([depth, 2, 2, 2, 128, 128], f32)
    small = np.zeros([128, NSMALL], f32)
    for l in range(depth):
        for fi, (g, u, d) in enumerate([("ffn1_wg", "ffn1_wu", "ffn1_wd"), ("ffn2_wg", "ffn2_wu", "ffn2_wd")]):
            base = C_GU1 if fi == 0 else C_GU2
            for j in range(NF):
                w8[l, base + 2 * j] = _ws(inp[g][l], j * 128, 8)
                w8[l, base + 2 * j + 1] = _ws(inp[u][l], j * 128, 8)
            for c in range(8):
                w22[l, fi * 8 + c] = _ws(inp[d][l], c * 128, NF)
        win = inp["w_in"][l]
        cols = [0, 128, 256, 384] + [512 + 128 * h for h in range(4)] + [1024 + 128 * h for h in range(4)] + [2048, 2176]
        for i, c0 in enumerate(cols):
            w8[l, C_IN + i] = _ws(win, c0, 8)
        for i in range(8):
            w8[l, C_OUT + i] = _ws(inp["w_out"][l], i * 128, 8)
        wv[l] = np.ascontiguousarray(win[:, 1536:2048].reshape(8, 128, 512).transpose(1, 0, 2)).reshape(128, 4096)
        for d_ in range(2):
            for ai, nm in enumerate(["rg_wa", "rg_wx"]):
                for c in range(2):
                    for hh in range(2):
                        wgate[l, d_, ai, c, hh * 64:(hh + 1) * 64, hh * 64:(hh + 1) * 64] = inp[nm][l, d_, 2 * c + hh]
        for i in range(3):
            small[:, SM_LNG + (l * 3 + i) * 8: SM_LNG + (l * 3 + i) * 8 + 8] = _col(inp["ln_g"][l, i])
            small[:, SM_LNB + (l * 3 + i) * 8: SM_LNB + (l * 3 + i) * 8 + 8] = _col(inp["ln_b"][l, i])
        for j in range(4):
            small[:, SM_CONVW + (l * 4 + j) * 2: SM_CONVW + (l * 4 + j) * 2 + 2] = _col(inp["conv_w"][l, j])
        small[:, SM_CONVB + l * 2: SM_CONVB + l * 2 + 2] = _col(inp["conv_b"][l])
        for d_ in range(2):
            o = (l * 2 + d_) * 2
            small[:, SM_BA + o: SM_BA + o + 2] = _col(inp["rg_ba"][l, d_])
            small[:, SM_BX + o: SM_BX + o + 2] = _col(inp["rg_bx"][l, d_])
            small[:, SM_LAM + o: SM_LAM + o + 2] = _col(inp["rg_lambda"][l, d_])
        small[:, SM_SUBG + l] = inp["subln_g"][l]
        small[:, SM_LQK + l * 256: SM_LQK + (l + 1) * 256] = inp["lambda_qk"][l].reshape(1, 256)
    c64 = np.zeros([128, 2, 512], np.float64)
    cc = np.arange(64)
    ang = 2 * np.pi * np.outer(cc, cc) / 64.0
    for j in range(2):
        for gl in range(2):
            g = 2 * j + gl
            c64[gl * 64:(gl + 1) * 64, j, g * 64:(g + 1) * 64] = -np.cos(ang)
            c64[gl * 64:(gl + 1) * 64, j, 256 + g * 64:256 + (g + 1) * 64] = np.sin(ang)
    t = np.arange(Smax)
    thi, tlo = (t // 128) * 128.0, (t % 128) * 1.0
    kext = np.zeros([4, 4, Smax], f32)
    qext = np.zeros([2, 4, 4, Smax], f32)
    for h in range(4):
        s8 = 8.0 * SLOPES[h]
        kext[h] = np.stack([s8 * thi, s8 * tlo, np.ones(Smax), np.ones(Smax)])
        qext[0, h] = np.stack([np.ones(Smax), np.ones(Smax), -s8 * thi, -s8 * tlo])
        qext[1, h] = -qext[0, h]
    ik = np.arange(128)[:, None]
    xx = np.arange(896)[None, :]
    dbig = np.abs(xx - 384 - ik).astype(f32)
    kio = np.stack([(t % 64) * 1.0, (t // 64) * 1.0]).astype(f32)
    tcol = np.zeros([128, 2, 2, 64], np.float64)
    for vi, S in enumerate(svars):
        A = S // 128
        for a in range(A):
            tt = 128 * a + np.arange(128)
            tcol[:, vi, 0, a] = ((64 * tt) % S) / S
            tcol[:, vi, 1, a] = tt / S
    return dict(w8=w8, w22=w22, wv=wv, wgate=wgate, small=small, c64=c64.reshape(128, 1024).astype(f32),
                kext=kext, qext=qext, dbig=dbig, kio=kio, tcol=tcol.reshape(128, 256).astype(f32))


def to_fm(x):
    S = x.shape[0]
    return np.ascontiguousarray(x.T).reshape(8, 128, S)


def from_fm(y):
    return np.ascontiguousarray(y.reshape(1024, -1).T)


SEQ_LENS = [4096, 4096, 8192]


def kernel(**inputs):
    inp = {k: np.asarray(v) for k, v in inputs.items()}
    xp, xs = inp["x_prompt"], inp["x_sample"]
    n_cores = 8
    nc = build_program(SEQ_LENS)
    sh = prep_shared(inp, DEPTH, max(SEQ_LENS), sorted(set(SEQ_LENS)))
    in_maps = []
    for c in range(n_cores):
        m = dict(sh)
        m["x0"] = to_fm(xp[2 * c])
        m["x1"] = to_fm(xp[2 * c + 1])
        m["x2"] = to_fm(xs[c]) if c < xs.shape[0] else np.zeros([8, 128, SEQ_LENS[2]], np.float32)
        in_maps.append(m)
    res = run_bass_kernel_spmd(nc, in_maps, core_ids=list(range(n_cores)))
    y_prompt = np.stack([from_fm(res.results[c][f"y{i}"]) for c in range(n_cores) for i in range(2)])
    y_sample = np.stack([from_fm(res.results[c]["y2"]) for c in range(xs.shape[0])])
    return (y_prompt.astype(np.float32), y_sample.astype(np.float32))
```

```python
import math
from contextlib import ExitStack
import numpy as np
import ml_dtypes
import concourse.bass as bass
import concourse.mybir as mybir
from concourse.bass_utils import run_bass_kernel_spmd

F32, BF16 = mybir.dt.float32, mybir.dt.bfloat16
ALU = mybir.AluOpType
AF = mybir.ActivationFunctionType

D = 1024
DFF = 2816
NF = DFF // 128
DEPTH = 2
T = 512
ALPHA = (2.0 * DEPTH) ** 0.25
LN_EPS = 1e-5
NORM_EPS = 1e-5
RG_C = 8.0
BG_BATCH = 8
MAGIC = 12582912.0
SLOPES = [2.0 ** (-8.0 * (h + 1) / 4) for h in range(4)]
C_GU1, C_GU2, C_IN, C_OUT, NCH8 = 0, 44, 88, 102, 110
SM_LNG = 0
SM_LNB = 48
SM_CONVW = 96
SM_CONVB = 112
SM_BA = 116
SM_BX = 124
SM_LAM = 132
SM_SUBG = 140
SM_LQK = 142
NSMALL = 142 + 512


class Buf:
    __slots__ = ("t", "lw", "rd", "excl")

    def __init__(self, t):
        self.t = t
        self.lw = None
        self.rd = {}
        self.excl = False

    def __getitem__(self, k):
        return self.t[k]


class PBuf(Buf):
    __slots__ = ("i",)

    def __init__(self, P, t, i):
        Buf.__init__(self, t)
        self.i = i
        self.excl = True
        P.bufs.append(self)

    def __getitem__(self, k):
        assert isinstance(k, slice) and k == slice(None), k
        return self.t[:, self.i * 512:(self.i + 1) * 512]

    def part(self, p0, p1):
        return self.t[p0:p1, self.i * 512:(self.i + 1) * 512]


class Prog:
    ENG = ("pe", "act", "dve", "pool", "sp")

    def __init__(self, nc, ndma=24):
        self.nc = nc
        self.ops = {e: [] for e in self.ENG}
        self.cnt = {e: 0 for e in self.ENG}
        self.seen = {e: {} for e in self.ENG}
        self.bufs = []
        self.ndma = ndma
        self.dtot = [0] * ndma
        self.drr = 0
        self.sb_off = 16512
        self.n_alloc = 0

    def sb(self, name, shape, dtype):
        esz = 2 if dtype == BF16 else 4
        n = 1
        for s in shape[1:]:
            n *= s
        nbytes = (n * esz + 63) // 64 * 64
        self.n_alloc += 1
        t = self.nc.alloc_sbuf_tensor_at(f"{name}_{self.n_alloc}", list(shape), dtype, offset=self.sb_off)
        self.sb_off += nbytes
        assert self.sb_off <= 228000, f"SBUF overflow at {name}: {self.sb_off}"
        b = Buf(t)
        self.bufs.append(b)
        return b

    def wrap(self, t):
        b = Buf(t)
        self.bufs.append(b)
        return b

    def _need(self, eng, tok, waits):
        if tok is None:
            return
        if tok[0] == "c":
            key, val = tok[1], tok[2]
            if key == eng and eng == "pe":
                return
        else:
            key, val = ("d", tok[1]), tok[2]
        if self.seen[eng].get(key, 0) >= val:
            return
        if waits.get(key, 0) < val:
            waits[key] = val

    def op(self, eng, fn, r=(), w=(), dma=False):
        waits = {}
        for b in r:
            self._need(eng, b.lw, waits)
            if b.excl:
                for k_, tok in b.rd.items():
                    if k_ != eng:
                        self._need(eng, tok, waits)
        for b in w:
            self._need(eng, b.lw, waits)
            for tok in b.rd.values():
                self._need(eng, tok, waits)
        if dma:
            j = self.drr
            self.drr = (self.drr + 1) % self.ndma
            if self.dtot[j] > 0:
                self._need(eng, ("d", j, self.dtot[j]), waits)
            self.dtot[j] += 16
            tok = ("d", j, self.dtot[j])
            key = ("d", j)
        else:
            self.cnt[eng] += 1
            tok = ("c", eng, self.cnt[eng])
            key = eng
        for k, v in waits.items():
            self.seen[eng][k] = v
        self.ops[eng].append((fn, tuple(waits.items()), key))
        for b in r:
            b.rd[key] = tok
        for b in w:
            b.lw = tok
            b.rd = {}

    def barrier(self):
        for eng in self.ENG:
            waits = {}
            for e2 in self.ENG:
                if e2 != eng and self.cnt[e2] > 0:
                    self._need(eng, ("c", e2, self.cnt[e2]), waits)
            for j in range(self.ndma):
                if self.dtot[j] > 0:
                    self._need(eng, ("d", j, self.dtot[j]), waits)
            for k, v in waits.items():
                self.seen[eng][k] = v
            self.ops[eng].append((None, tuple(waits.items()), None))
        for b in self.bufs:
            b.lw = None
            b.rd = {}

    def emit(self):
        nc = self.nc
        with ExitStack() as st:
            csem = {e: st.enter_context(nc.semaphore(f"c_{e}")) for e in self.ENG}
            dsem = [st.enter_context(nc.semaphore(f"d_{j}")) for j in range(self.ndma)]
            block = st.enter_context(nc.Block())

            def run(name):
                ops = self.ops[name]

                def f(e):
                    for fn, waits, key in ops:
                        for k, v in waits:
                            e.wait_ge(dsem[k[1]] if isinstance(k, tuple) else csem[k], v)
                        if fn is None:
                            continue
                        ins = fn(e)
                        if isinstance(key, tuple):
                            ins.then_inc(dsem[key[1]], 16)
                        else:
                            ins.then_inc(csem[key], 1)
                return f

            block.tensor(run("pe"))
            block.scalar(run("act"))
            block.vector(run("dve"))
            block.gpsimd(run("pool"))
            block.sync(run("sp"))


def MM(P, psb, out, lhsT, rhs, start, stop, rd):
    P.op("pe", lambda e: e.matmul(out, lhsT, rhs, start=start, stop=stop), r=rd, w=[psb])


def ACTV(P, out, in_, func, r, w, bias=0.0, scale=1.0):
    P.op("act", lambda e: e.activation(out=out, in_=in_, func=func, bias=bias, scale=scale), r=r, w=w)


def TT(P, eng, out, in0, in1, op, r, w):
    P.op(eng, lambda e: e.tensor_tensor(out=out, in0=in0, in1=in1, op=op), r=r, w=w)


def TS(P, eng, out, in0, s1, s2, op0, op1, r, w):
    if s2 is None:
        P.op(eng, lambda e: e.tensor_scalar(out=out, in0=in0, scalar1=s1, scalar2=None, op0=op0), r=r, w=w)
    else:
        P.op(eng, lambda e: e.tensor_scalar(out=out, in0=in0, scalar1=s1, scalar2=s2, op0=op0, op1=op1), r=r, w=w)


def STT(P, eng, out, in0, scalar, in1, op0, op1, r, w):
    eng = "dve"
    P.op(eng, lambda e: e.scalar_tensor_tensor(out=out, in0=in0, scalar=scalar, in1=in1, op0=op0, op1=op1), r=r, w=w)


def CP(P, eng, out, in_, r, w):
    if eng == "act":
        P.op("act", lambda e: e.activation(out=out, in_=in_, func=AF.Copy), r=r, w=w)
    else:
        P.op(eng, lambda e: e.tensor_copy(out=out, in_=in_), r=r, w=w)


def RECIP(P, out, in_, r, w):
    P.op("dve", lambda e: e.reciprocal(out=out, in_=in_), r=r, w=w)


def DMA(P, out, in_, r=(), w=(), eng="sp"):
    P.op(eng, lambda e: e.dma_start(out=out, in_=in_), r=r, w=w, dma=True)


def MSET(P, eng, ap, val, w):
    P.op(eng, lambda e: e.memset(ap, val), r=(), w=w)


class Ctx:
    pass


def build_program(seq_lens, depth=DEPTH, debug=False, stop_after=None):
    nc = bass.Bass("TRN2", target_bir_lowering=False)
    P = Prog(nc)
    C = Ctx()
    C.P, C.nc, C.depth = P, nc, depth
    Smax = max(seq_lens)
    C.Smax = Smax
    skind = "ExternalOutput" if debug else "Internal"

    def dram(name, shape, dt, kind):
        return nc.dram_tensor(name, list(shape), dt, kind=kind).ap()

    C.xin = [dram(f"x{i}", [8, 128, S], F32, "ExternalInput") for i, S in enumerate(seq_lens)]
    C.yout = [dram(f"y{i}", [8, 128, S], F32, "ExternalOutput") for i, S in enumerate(seq_lens)]
    C.w8 = dram("w8", [depth, NCH8, 128, 1024], F32, "ExternalInput")
    C.w22 = dram("w22", [depth, 16, 128, DFF], F32, "ExternalInput")
    C.wv = dram("wv", [depth, 128, 8 * 512], F32, "ExternalInput")
    C.wgate = dram("wgate", [depth, 2, 2, 2, 128, 128], F32, "ExternalInput")
    C.small = dram("small", [128, NSMALL], F32, "ExternalInput")
    C.c64 = dram("c64", [128, 2 * 512], F32, "ExternalInput")
    C.kext = dram("kext", [4, 4, Smax], F32, "ExternalInput")
    C.qext = dram("qext", [2, 4, 4, Smax], F32, "ExternalInput")
    C.dbig = dram("dbig", [128, 896], F32, "ExternalInput")
    C.kio = dram("kio", [2, Smax], F32, "ExternalInput")
    C.tcol = dram("tcol", [128, 2 * 2 * 64], F32, "ExternalInput")
    C.w8b = dram("w8b", [depth, NCH8, 128, 1024], BF16, "Internal")
    C.w22b = dram("w22b", [depth, 16, 128, DFF], BF16, "Internal")
    C.x1T = dram("x1T", [8, 128, Smax], F32, skind)
    C.rxg = dram("rxg", [4, 128, Smax], F32, skind)
    C.qk = dram("qk", [8, 128, Smax], BF16, skind)
    C.G = dram("G", [Smax, 512], BF16, skind)
    C.v = dram("v", [Smax, 512], BF16, skind)
    C.yT = dram("yT", [8, 128, Smax], BF16, skind)
    C.dft = {}
    for S in sorted(set(seq_lens)):
        C.dft[S] = dram(f"dft{S}", [2, S // 128, 128, S], BF16, "Internal")

    C.ps2 = [nc.alloc_psum_tensor(f"psp{i}", [128, 1024], F32) for i in range(4)]
    C.ps = [PBuf(P, C.ps2[i // 2], i % 2) for i in range(8)]
    for b_ in C.ps:
        b_.excl = True

    C.smallt = P.sb("small", [128, NSMALL], F32)
    C.ones_ln = P.sb("ones_ln", [128, 128], BF16)
    C.ones1 = P.sb("ones1", [128, 128], BF16)
    C.ones_sub = P.sb("ones_sub", [128, 128], BF16)
    DMA(P, C.smallt[:], C.small[:, :], w=[C.smallt])
    MSET(P, "dve", C.ones_ln[:], 1.0 / 1024.0, [C.ones_ln])
    MSET(P, "dve", C.ones1[:], 1.0, [C.ones1])
    MSET(P, "dve", C.ones_sub[:], 1.0 / 128.0, [C.ones_sub])
    C.base_off = P.sb_off
    bg_alloc(C, min(2048, min(seq_lens)))

    setup_weights(C)
    P.barrier()
    for vi, S_ in enumerate(sorted(set(seq_lens))):
        for _ in dft_gen(C, S_, vi):
            pass
    P.barrier()
    if stop_after == "setup":
        P.emit()
        return nc
    for si, S in enumerate(seq_lens):
        for k in range(depth + 1):
            tl_phase(C, si, S, k)
            if k < depth:
                bg_drain(C, S)
            P.barrier()
            if stop_after == ("tl", si, k):
                P.emit()
                return nc
            if k < depth:
                mixer_phase(C, si, S, k)
                P.barrier()
    P.emit()
    return nc


def setup_weights(C):
    P = C.P
    for l in range(C.depth):
        for c0 in range(0, NCH8, 10):
            c1 = min(NCH8, c0 + 10)
            DMA(P, C.w8b[l, c0:c1], C.w8[l, c0:c1], eng="pool")
        for c0 in range(0, 16, 4):
            DMA(P, C.w22b[l, c0:c0 + 4], C.w22[l, c0:c0 + 4], eng="pool")


def sm(C, col, n=1):
    return C.smallt[:, col:col + n]


class CBuf:
    def __init__(self, P, name, shape, dtype):
        b = P.sb(name, shape, dtype)
        self.t = b.t
        self.c = [b] + [P.wrap(b.t) for _ in range(shape[1] - 1)]

    def __getitem__(self, k):
        return self.t[k]


def tl_phase(C, si, S, k):
    P, nc, depth = C.P, C.nc, C.depth
    P.sb_off = C.base_off
    has_out = k > 0
    has_in = k < depth
    ps = C.ps
    xfs = [CBuf(P, f"xf{i}", [128, 8, T], F32) for i in range(2)]
    xb = CBuf(P, "xb", [128, 8, T], BF16)
    sq = CBuf(P, "sq", [128, 8, T], BF16)
    hact = CBuf(P, "hact", [128, NF, T], BF16)
    sg = [P.sb(f"sg{i}", [128, T], F32) for i in range(2)]
    tmp = [P.sb(f"tmp{i}", [128, T], F32) for i in range(3)]
    mean = P.sb("mean", [128, T], F32)
    msq = P.sb("msq", [128, T], F32)
    rstd = P.sb("rstd", [128, T], F32)
    mr = P.sb("mr", [128, T], F32)
    dmy = P.sb("dmy", [128, 2], F32)
    MSET(P, "dve", dmy[:], 1.0, [dmy])

    def touch(func):
        ACTV(P, dmy[:, 1:2], dmy[:, 0:1], func, [dmy], [dmy])
    w8r = [P.sb(f"w8r{i}", [128, 8, 128], BF16) for i in range(7)]
    w22r = [P.sb(f"w22r{i}", [128, NF, 128], BF16) for i in range(4)]
    if has_out:
        ybs = [CBuf(P, f"yb{i}", [128, 8, T], BF16) for i in range(2)]
    if has_in:
        wvt = P.sb("wvt", [128, 8, 512], BF16)
        c64t = P.sb("c64t", [128, 2, 512], BF16)
        rxg_st = P.sb("rxg_st", [128, 4, T], F32)
        qk_st = P.sb("qk_st", [128, 8, T], BF16)
        fxT = P.sb("fxT", [128, 2, T], BF16)
        g_st = P.sb("g_st", [128, 4, 512], BF16)
        v_st = P.sb("v_st", [128, 4, 512], BF16)
        DMA(P, wvt[:].rearrange("p a b -> p (a b)"), C.wv[k], w=[wvt], eng="pool")
        DMA(P, c64t[:].rearrange("p a b -> p (a b)"), C.c64[:, :], w=[c64t], eng="pool")

    w8_state = {"i": 0}
    w22_state = {"i": 0}

    def ws_linear(l, chunks, rhs, epilogue, ps_ids, group_first=False):
        n = len(chunks)
        Dp = 4
        slots = {}
        nxt = {"i": 0}

        def load_upto(m):
            while nxt["i"] < min(m, n):
                i = nxt["i"]
                nxt["i"] += 1
                s_ = w8_state["i"] % len(w8r)
                w8_state["i"] += 1
                slots[i] = w8r[s_]
                DMA(P, w8r[s_][:].rearrange("p a b -> p (a b)"), C.w8b[l, chunks[i]], w=[w8r[s_]])
        load_upto(Dp)
        i = 0
        if group_first and n >= 4:
            load_upto(6)
            for kc in range(8):
                for g in range(4):
                    pb = ps[ps_ids[g % len(ps_ids)]]
                    MM(P, pb, pb[:], slots[g][:, kc, :], rhs[:, kc, :], kc == 0, kc == 7, [slots[g], rhs.c[kc]])
            for g in range(4):
                slots.pop(g)
                epilogue(g, ps[ps_ids[g % len(ps_ids)]])
            i = 4
        while i < n:
            load_upto(i + Dp + 1)
            wt = slots.pop(i)
            pb = ps[ps_ids[i % len(ps_ids)]]
            for kc in range(8):
                MM(P, pb, pb[:], wt[:, kc, :], rhs[:, kc, :], kc == 0, kc == 7, [wt, rhs.c[kc]])
            epilogue(i, pb)
            bg_step(C)
            i += 1

    def stats_prep(c):
        xf = cur["xf"]
        CP(P, "dve", xb[:, c, :], xf[:, c, :], [xf.c[c]], [xb.c[c]])
        ACTV(P, sq[:, c, :], xf[:, c, :], AF.Square, [xf.c[c]], [sq.c[c]])

    def stats_mm(c):
        MM(P, ps[6], ps[6][:], C.ones_ln[:], xb[:, c, :], c == 0, c == 7, [C.ones_ln, xb.c[c]])
        MM(P, ps[7], ps[7][:], C.ones_ln[:], sq[:, c, :], c == 0, c == 7, [C.ones_ln, sq.c[c]])

    def ffn(l, cbase, w22base, ln_idx):
        xf = cur["xf"]
        def ep(i, pb):
            j = i // 2
            if i % 2 == 0:
                ACTV(P, sg[j % 2][:], pb[:], AF.Silu, [pb], [sg[j % 2]])
            else:
                TT(P, "dve", hact[:, j, :], sg[j % 2][:], pb[:], ALU.mult, [sg[j % 2], pb], [hact.c[j]])
        ws_linear(l, [cbase + i for i in range(2 * NF)], xb, ep, [0, 1, 2, 3], group_first=True)
        touch(AF.Ln)
        slots = {}

        def load(c):
            s_ = w22_state["i"] % len(w22r)
            w22_state["i"] += 1
            slots[c] = w22r[s_]
            DMA(P, w22r[s_][:].rearrange("p a b -> p (a b)"), C.w22b[l, w22base + c], w=[w22r[s_]])
        load(0)
        load(1)
        for c in range(8):
            if c + 2 < 8:
                load(c + 2)
            wt = slots.pop(c)
            pb = ps[4 + c % 2]
            for kc in range(NF):
                MM(P, pb, pb[:], wt[:, kc, :], hact[:, kc, :], kc == 0, kc == NF - 1, [wt, hact.c[kc]])
            STT(P, "dve", xf[:, c, :], pb[:], 0.5 / ALPHA, xf[:, c, :], ALU.mult, ALU.add, [pb, xf.c[c]], [xf.c[c]])
            stats_prep(c)
            if c > 0:
                stats_mm(c - 1)
            bg_step(C)
        stats_mm(7)
        layer_norm(l, ln_idx, next_silu=(ln_idx == 2 and has_in))

    def layer_norm(l, idx, next_silu=False):
        eps = LN_EPS / (ALPHA * ALPHA)
        xf = cur["xf"]
        CP(P, "dve", mean[:], ps[6][:], [ps[6]], [mean])
        TT(P, "dve", msq[:], mean[:], mean[:], ALU.mult, [mean], [msq])
        STT(P, "dve", rstd[:], ps[7][:], eps, msq[:], ALU.add, ALU.subtract, [ps[7], msq], [rstd])
        ACTV(P, rstd[:], rstd[:], AF.Ln, [rstd], [rstd])
        ACTV(P, rstd[:], rstd[:], AF.Exp, [rstd], [rstd], scale=-0.5)
        if next_silu:
            touch(AF.Silu)
        gcol = SM_LNG + (l * 3 + idx) * 8
        bcol = SM_LNB + (l * 3 + idx) * 8
        for c in range(8):
            tb = tmp[c % 3]
            TT(P, "dve", tb[:], xf[:, c, :], mean[:], ALU.subtract, [xf.c[c], mean], [tb])
            TT(P, "dve", tb[:], tb[:], rstd[:], ALU.mult, [tb, rstd], [tb])
            ACTV(P, xb[:, c, :], tb[:], AF.Identity, [tb, C.smallt], [xb.c[c]],
                 bias=sm(C, bcol + c), scale=sm(C, gcol + c))
            ACTV(P, xf[:, c, :], tb[:], AF.Identity, [tb, C.smallt], [xf.c[c]],
                 bias=sm(C, bcol + c), scale=sm(C, gcol + c))

    nt = S // T
    IOQ = "pool"
    cur = {}

    def load_tile(ti):
        tsl_ = slice(ti * T, (ti + 1) * T)
        xf_ = xfs[ti % 2]
        if k == 0:
            DMA(P, xf_[:], C.xin[si][:, :, tsl_].rearrange("c p t -> p c t"), w=xf_.c, eng=IOQ)
        else:
            DMA(P, xf_[:], C.x1T[:, :, tsl_].rearrange("c p t -> p c t"), w=xf_.c, eng=IOQ)
            DMA(P, ybs[ti % 2][:], C.yT[:, :, tsl_].rearrange("c p t -> p c t"), w=ybs[ti % 2].c, eng=IOQ)
    load_tile(0)
    for ti in range(nt):
        tsl = slice(ti * T, (ti + 1) * T)
        if ti + 1 < nt:
            load_tile(ti + 1)
        xf = xfs[ti % 2]
        cur["xf"] = xf
        if has_out:
            yb = ybs[ti % 2]
        if k == 0:
            for c in range(8):
                CP(P, "dve" if c % 2 else "act", xb[:, c, :], xf[:, c, :], [xf.c[c]], [xb.c[c]])
        if has_out:
            lo = k - 1

            def ep_o(i, pb):
                STT(P, "dve", xf[:, i, :], pb[:], 1.0 / ALPHA, xf[:, i, :], ALU.mult, ALU.add, [pb, xf.c[i]], [xf.c[i]])
                stats_prep(i)
                if i > 0:
                    stats_mm(i - 1)
            touch(AF.Ln)
            ws_linear(lo, [C_OUT + i for i in range(8)], yb, ep_o, [0, 1, 2, 3])
            stats_mm(7)
            layer_norm(lo, 1, next_silu=True)
            ffn(lo, C_GU2, 8, 2)
        if not has_in:
            DMA(P, C.yout[si][:, :, tsl].rearrange("c p t -> p c t"), xf[:], r=xf.c, eng=IOQ)
            continue
        ffn(k, C_GU1, 0, 0)
        DMA(P, C.x1T[:, :, tsl].rearrange("c p t -> p c t"), xf[:], r=xf.c, eng=IOQ)

        def ep_i(i, pb):
            if i < 4:
                CP(P, "act", rxg_st[:, i, :], pb[:], [pb], [rxg_st])
            elif i < 12:
                CP(P, "dve" if i % 2 else "act", qk_st[:, i - 4, :], pb[:], [pb], [qk_st])
            else:
                CP(P, "dve", fxT[:, i - 12, :], pb[:], [pb], [fxT])
        ws_linear(k, [C_IN + i for i in range(14)], xb, ep_i, [0, 1, 2, 3], group_first=True)
        DMA(P, C.rxg[:, :, tsl].rearrange("c p t -> p c t"), rxg_st[:], r=[rxg_st], eng=IOQ)
        DMA(P, C.qk[:, :, tsl].rearrange("c p t -> p c t"), qk_st[:], r=[qk_st], eng=IOQ)
        for tb in range(4):
            pb = ps[4 + tb % 2]
            for j in range(2):
                MM(P, pb, pb[:], fxT[:, j, tb * 128:(tb + 1) * 128], c64t[:, j, :], j == 0, j == 1, [fxT, c64t])
            CP(P, "act", g_st[:, tb, :], pb[:], [pb], [g_st])
        DMA(P, C.G[tsl, :].rearrange("(b p) f -> p b f", p=128), g_st[:], r=[g_st], eng=IOQ)
        for tb in range(4):
            pb = ps[tb % 4]
            for kc in range(8):
                MM(P, pb, pb[:], xb[:, kc, tb * 128:(tb + 1) * 128], wvt[:, kc, :], kc == 0, kc == 7, [xb.c[kc], wvt])
            CP(P, "dve", v_st[:, tb, :], pb[:], [pb], [v_st])
        DMA(P, C.v[tsl, :].rearrange("(b p) f -> p b f", p=128), v_st[:], r=[v_st], eng=IOQ)


def bg_alloc(C, W):
    P = C.P
    B = Ctx()
    B.W = W
    B.klo = P.sb("bg_klo", [128, W], F32)
    B.khi = P.sb("bg_khi", [128, W], F32)
    B.tct = P.sb("bg_tct", [128, 256], F32)
    B.c1j = P.sb("bg_c1j", [128, 16, 64], F32)
    B.base = P.sb("bg_base", [128, W], F32)
    B.pp = [P.sb(f"bg_pp{i}", [128, W], F32) for i in range(2)]
    B.rn = [P.sb(f"bg_rn{i}", [128, W], F32) for i in range(2)]
    B.ab = [P.sb(f"bg_ab{i}", [128, W], F32) for i in range(2)]
    B.oc = [P.sb(f"bg_oc{i}", [128, W], BF16) for i in range(2)]
    B.os = [P.sb(f"bg_os{i}", [128, W], BF16) for i in range(2)]
    DMA(P, B.klo[:], C.kio[0, 0:W].partition_broadcast(128), w=[B.klo])
    DMA(P, B.khi[:], C.kio[1, 0:W].partition_broadcast(128), w=[B.khi])
    DMA(P, B.tct[:], C.tcol[:, :], w=[B.tct])
    C.B = B
    C.bg = []
    C.bg_calls = 0


def dft_gen(C, S, vi):
    P, B = C.P, C.B
    W = B.W
    A = S // 128
    nj = S // W
    for j in range(nj):
        TS(P, "dve", B.c1j[:, j, :], B.tct[:, vi * 128: vi * 128 + 64], (W // 64) * 1.0 * j, None, ALU.mult, None,
           [B.tct], [B.c1j])
    it = 0
    for a in range(A):
        c1 = B.tct[:, vi * 128 + a: vi * 128 + a + 1]
        c2 = B.tct[:, vi * 128 + 64 + a: vi * 128 + 64 + a + 1]
        TS(P, "dve", B.base[:], B.klo[:], c2, None, ALU.mult, None, [B.klo, B.tct], [B.base])
        STT(P, "dve", B.base[:], B.khi[:], c1, B.base[:], ALU.mult, ALU.add, [B.khi, B.tct, B.base], [B.base])
        for j in range(nj):
            b = it % 2
            it += 1
            sl = slice(j * W, (j + 1) * W)
            pp, rn, ab, oc, os_ = B.pp[b], B.rn[b], B.ab[b], B.oc[b], B.os[b]
            TS(P, "dve", pp[:], B.base[:], B.c1j[:, j, a:a + 1], None, ALU.add, None, [B.base, B.c1j], [pp])
            TS(P, "dve", rn[:], pp[:], MAGIC, MAGIC, ALU.add, ALU.subtract, [pp], [rn])
            TT(P, "dve", pp[:], pp[:], rn[:], ALU.subtract, [pp, rn], [pp])
            ACTV(P, os_[:], pp[:], AF.Sin, [pp], [os_], scale=-6.28318)
            ACTV(P, ab[:], pp[:], AF.Abs, [pp], [ab])
            ACTV(P, oc[:], ab[:], AF.Sin, [ab], [oc], scale=6.28318, bias=-1.570796)
            DMA(P, C.dft[S][0, a, :, sl], oc[:], r=[oc])
            DMA(P, C.dft[S][1, a, :, sl], os_[:], r=[os_])
            yield


def bg_step(C):
    if not C.bg:
        return
    C.bg_calls += 1
    S_, g, every = C.bg[0]
    if C.bg_calls % every:
        return
    try:
        for _ in range(BG_BATCH):
            next(g)
    except StopIteration:
        C.bg.pop(0)


def bg_drain(C, S):
    while any(e[0] == S for e in C.bg):
        S_, g, every = C.bg[0]
        for _ in g:
            pass
        C.bg.pop(0)


def run_interleaved(gens):
    gens = list(gens)
    while gens:
        for g in list(gens):
            try:
                next(g)
            except StopIteration:
                gens.remove(g)


def mixer_phase(C, si, S, l):
    C.P.sb_off = C.base_off
    run_interleaved([rg_phase(C, S, l), fnet_phase(C, S, l)])
    C.P.barrier()
    attn_phase(C, S, l)


def rg_phase(C, S, l):
    P, ps = C.P, C.ps
    TT_ = min(S, 2048) if S <= 4096 else 1024
    ntt = S // TT_
    sets = []
    for i in range(2):
        sets.append(dict(
            rxp=P.sb(f"rxp{i}", [128, TT_ + 3], F32), xc=P.sb(f"xc{i}", [128, TT_], F32),
            xcb=P.sb(f"xcb{i}", [128, TT_], BF16), A=P.sb(f"rgA{i}", [128, TT_], F32),
            U=P.sb(f"rgU{i}", [128, TT_], F32), G1=P.sb(f"rgG{i}", [128, TT_], F32),
            H2=P.sb(f"rgH2{i}", [128, TT_], F32), ob=P.sb(f"rgob{i}", [128, TT_], BF16)))
    Hf = P.sb("rgHf", [128, S], F32)
    step = 0
    wgt = [[P.sb(f"wg{d}{a}", [128, 128], BF16) for a in range(2)] for d in range(2)]
    kap = P.sb("kap", [128, 4], F32)
    carry = P.sb("carry", [128, 1], F32)
    for c in range(2):
        for d in range(2):
            for a in range(2):
                DMA(P, wgt[d][a][:], C.wgate[l, d, a, c], w=[wgt[d][a]], eng="pool")
            lamc = sm(C, SM_LAM + (l * 2 + d) * 2 + c)
            ACTV(P, kap[:, d:d + 1], lamc, AF.Exp, [C.smallt], [kap], scale=-1.0)
            ACTV(P, kap[:, d:d + 1], kap[:, d:d + 1], AF.Ln, [kap], [kap], bias=1.0)
            TS(P, "dve", kap[:, d:d + 1], kap[:, d:d + 1], -RG_C, None, ALU.mult, None, [kap], [kap])
        for d in range(2):
            order = list(range(ntt)) if d == 0 else list(range(ntt - 1, -1, -1))
            for oi, tt in enumerate(order):
                B_ = sets[step % 2]
                step += 1
                rxp, xc, xcb, A, U, G1, H2, ob = (B_[k_] for k_ in ("rxp", "xc", "xcb", "A", "U", "G1", "H2", "ob"))
                t0 = tt * TT_
                lo, hi = t0 - 2, t0 + TT_ + 1
                dlo, slo = 0, lo
                if lo < 0:
                    MSET(P, "dve", rxp[:, 0:2], 0.0, [rxp])
                    dlo, slo = 2, 0
                shi = hi
                if hi > S:
                    MSET(P, "dve", rxp[:, TT_ + 2:TT_ + 3], 0.0, [rxp])
                    shi = S
                DMA(P, rxp[:, dlo:dlo + (shi - slo)], C.rxg[c, :, slo:shi], w=[rxp])
                TS(P, "dve", xc[:], rxp[:, 0:TT_], sm(C, SM_CONVW + (l * 4 + 0) * 2 + c), sm(C, SM_CONVB + l * 2 + c),
                   ALU.mult, ALU.add, [rxp, C.smallt], [xc])
                for j in range(1, 4):
                    STT(P, "dve", xc[:], rxp[:, j:j + TT_], sm(C, SM_CONVW + (l * 4 + j) * 2 + c), xc[:],
                        ALU.mult, ALU.add, [rxp, xc, C.smallt], [xc])
                CP(P, "act", xcb[:], xc[:], [xc], [xcb])
                yield
                bac = sm(C, SM_BA + (l * 2 + d) * 2 + c)
                bxc = sm(C, SM_BX + (l * 2 + d) * 2 + c)
                for b in range(TT_ // 512):
                    bs = slice(b * 512, (b + 1) * 512)
                    pr, pi = ps[(b % 2) * 2], ps[(b % 2) * 2 + 1]
                    MM(P, pr, pr[:], wgt[d][0][:], xcb[:, bs], True, True, [wgt[d][0], xcb])
                    MM(P, pi, pi[:], wgt[d][1][:], xcb[:, bs], True, True, [wgt[d][1], xcb])
                    ACTV(P, A[:, bs], pr[:], AF.Sigmoid, [pr, C.smallt], [A], bias=bac)
                    ACTV(P, G1[:, bs], pi[:], AF.Sigmoid, [pi, C.smallt], [G1], bias=bxc)
                    yield
                ACTV(P, A[:], A[:], AF.Exp, [A, kap], [A], scale=kap[:, d:d + 1])
                ACTV(P, U[:], A[:], AF.Square, [A], [U])
                ACTV(P, U[:], U[:], AF.Sqrt, [U], [U], scale=-1.0, bias=1.0)
                TT(P, "dve", U[:], U[:], G1[:], ALU.mult, [U, G1], [U])
                TT(P, "dve", U[:], U[:], xc[:], ALU.mult, [U, xc], [U])
                yield
                CH = min(TT_, 2048)
                nch = TT_ // CH
                if d == 0:
                    for q in range(nch):
                        g0 = t0 + q * CH
                        init = 0.0 if g0 == 0 else Hf[:, g0 - 1:g0]
                        P.op("dve", lambda e, o=Hf[:, g0:g0 + CH], a_=A[:, q * CH:(q + 1) * CH], u_=U[:, q * CH:(q + 1) * CH], i_=init:
                             e.tensor_tensor_scan(out=o, data0=a_, data1=u_, initial=i_, op0=ALU.mult, op1=ALU.add),
                             r=[A, U, Hf], w=[Hf])
                else:
                    for q in range(nch - 1, -1, -1):
                        first = (oi == 0 and q == nch - 1)
                        init = 0.0 if first else carry[:, 0:1]
                        sl = slice(q * CH, (q + 1) * CH)
                        P.op("dve", lambda e, o=H2[:, sl][:, ::-1], a_=A[:, sl][:, ::-1], u_=U[:, sl][:, ::-1], i_=init:
                             e.tensor_tensor_scan(out=o, data0=a_, data1=u_, initial=i_, op0=ALU.mult, op1=ALU.add),
                             r=[A, U, carry], w=[H2])
                        CP(P, "dve", carry[:, 0:1], H2[:, q * CH:q * CH + 1], [H2], [carry])
                    yield
                    DMA(P, rxp[:, 0:TT_], C.rxg[2 + c, :, t0:t0 + TT_], w=[rxp])
                    rg = rxp[:, 0:TT_]
                    ACTV(P, G1[:], rg, AF.Square, [rxp], [G1])
                    TS(P, "dve", G1[:], G1[:], 0.044715, 1.0, ALU.mult, ALU.add, [G1], [G1])
                    TT(P, "dve", G1[:], G1[:], rg, ALU.mult, [G1, rxp], [G1])
                    ACTV(P, G1[:], G1[:], AF.Sigmoid, [G1], [G1], scale=1.5957691216)
                    TT(P, "dve", G1[:], G1[:], rg, ALU.mult, [G1, rxp], [G1])
                    TT(P, "dve", H2[:], H2[:], Hf[:, t0:t0 + TT_], ALU.add, [H2, Hf], [H2])
                    TT(P, "dve", ob[:], H2[:], G1[:], ALU.mult, [H2, G1], [ob])
                    DMA(P, C.yT[c, :, t0:t0 + TT_], ob[:], r=[ob])
                yield


def attn_phase(C, S, l):
    P, ps = C.P, C.ps
    P.sb_off = C.base_off
    nkb = S // 128
    nqt = S // 512
    lam_init = 0.8 - 0.6 * math.exp(-0.3 * l)
    kxs = [[P.sb(f"kx{i}{m}", [128, S], BF16) for m in range(2)] for i in range(2)]
    vts = [P.sb(f"vt{i}", [128, nkb, 128], BF16) for i in range(2)]
    qa = [[P.sb(f"qa{i}{m}", [128, 512], BF16) for m in range(2)] for i in range(2)]
    qb = [[P.sb(f"qb{i}{m}", [128, 512], BF16) for m in range(2)] for i in range(2)]
    dbt = P.sb("dbt", [128, 896], F32)
    pT = [P.sb(f"pT{i}", [128, 2, 512], BF16) for i in range(8)]
    sbias = [P.sb(f"sbias{i}", [128, 2, 512], F32) for i in range(4)]
    l1s = P.sb("al1s", [128, 512], F32)
    accL = [P.sb(f"accL{i}", [128, 2, 512], F32) for i in range(2)]
    osb = [P.sb(f"osb{m}", [128, 512], F32) for m in range(2)]
    r0 = P.sb("ar0", [128, 512], F32)
    t0_ = P.sb("at0", [128, 512], F32)
    t1_ = P.sb("at1", [128, 512], F32)
    oo = P.sb("aoo", [128, 512], F32)
    osq = P.sb("aosq", [128, 512], BF16)
    rs = P.sb("ars", [128, 512], F32)
    obuf = [P.sb(f"aob{i}", [128, 512], BF16) for i in range(2)]
    lq = P.sb("lq", [128, 128], F32)
    sc = P.sb("asc", [128, 8], F32)
    ones_f = P.sb("ones_f", [128, 128], F32)
    MSET(P, "dve", ones_f[:], 1.0, [ones_f])
    DMA(P, dbt[:], C.dbig[:, :], w=[dbt])
    base = SM_LQK + l * 256
    TT(P, "dve", lq[:, 0:64], sm(C, base, 64), sm(C, base + 64, 64), ALU.mult, [C.smallt], [lq])
    TT(P, "dve", lq[:, 64:128], sm(C, base + 128, 64), sm(C, base + 192, 64), ALU.mult, [C.smallt], [lq])
    P.op("dve", lambda e: e.reduce_sum(out=sc[:, 0:1], in_=lq[:, 0:64], axis=mybir.AxisListType.X), r=[lq], w=[sc])
    P.op("dve", lambda e: e.reduce_sum(out=sc[:, 1:2], in_=lq[:, 64:128], axis=mybir.AxisListType.X), r=[lq], w=[sc])
    ACTV(P, sc[:, 2:4], sc[:, 0:2], AF.Exp, [sc], [sc])
    TT(P, "dve", sc[:, 4:5], sc[:, 3:4], sc[:, 2:3], ALU.subtract, [sc], [sc])
    TS(P, "dve", sc[:, 4:5], sc[:, 4:5], -lam_init, None, ALU.add, None, [sc], [sc])
    TS(P, "dve", sc[:, 5:6], sm(C, SM_SUBG + l), 1.0 - lam_init, None, ALU.mult, None, [C.smallt], [sc])
    neglam = sc[:, 4:5]
    gsc = sc[:, 5:6]
    it = 0
    pend_tail = []
    drate = 3 if nkb < 16 else 1
    def load_kv(h_):
        kx_, vt_ = kxs[h_ % 2], vts[h_ % 2]
        for m in range(2):
            DMA(P, kx_[m][0:64, :], C.qk[4 + h_, m * 64:(m + 1) * 64, 0:S], w=[kx_[m]])
            DMA(P, kx_[m][64:68, :], C.kext[h_, :, 0:S], w=[kx_[m]], eng="pool")
        DMA(P, vt_[:], C.v[0:S, h_ * 128:(h_ + 1) * 128].rearrange("(b p) e -> p b e", p=128), w=[vt_])
    load_kv(0)
    for h in range(4):
        s8 = 8.0 * SLOPES[h]
        kx, vt = kxs[h % 2], vts[h % 2]
        if h + 1 < 4:
            load_kv(h + 1)

        def load_q(qt):
            qs = slice(qt * 512, qt * 512 + 512)
            qi = qt % 2
            for m in range(2):
                DMA(P, qa[qi][m][0:64, :], C.qk[h, m * 64:(m + 1) * 64, qs], w=[qa[qi][m]])
                DMA(P, qa[qi][m][64:68, :], C.qext[0, h, :, qs], w=[qa[qi][m]], eng="pool")
                DMA(P, qb[qi][m][0:64, :], C.qk[h, m * 64:(m + 1) * 64, qs], w=[qb[qi][m]])
                DMA(P, qb[qi][m][64:68, :], C.qext[1, h, :, qs], w=[qb[qi][m]], eng="pool")
        load_q(0)
        for qt in range(nqt):
            Q0 = qt * 512
            qs = slice(Q0, Q0 + 512)
            qi = qt % 2
            if qt + 1 < nqt:
                load_q(qt + 1)
            al = accL[qt % 2]

            def qk_mm(kb, itn):
                K0 = kb * 128
                ks = slice(K0, K0 + 128)
                srcs = []
                for m in range(2):
                    sp = ps[(itn % 2) * 2 + m]
                    if K0 + 128 <= Q0:
                        MM(P, sp, sp[:], kx[m][0:68, ks], qa[qi][m][0:68, :], True, True, [kx[m], qa[qi][m]])
                        srcs.append((sp[:], sp, None))
                    elif K0 >= Q0 + 512:
                        MM(P, sp, sp[:], kx[m][0:68, ks], qb[qi][m][0:68, :], True, True, [kx[m], qb[qi][m]])
                        srcs.append((sp[:], sp, None))
                    else:
                        j = (K0 - Q0) // 128
                        MM(P, sp, sp[:], kx[m][0:64, ks], qa[qi][m][0:64, :], True, True, [kx[m], qa[qi][m]])
                        sbb = sbias[itn % 4]
                        STT(P, "dve", sbb[:, m, :], dbt[:, 384 - 128 * j: 384 - 128 * j + 512], -s8, sp[:],
                            ALU.mult, ALU.add, [dbt, sp], [sbb])
                        srcs.append((sp[:], sp, j))
                return srcs
            pend = {0: qk_mm(0, it)}
            if nkb > 1:
                pend[1] = qk_mm(1, it + 1)
            for kb in range(nkb):
                cur = pend.pop(kb)
                itc = it
                pslot = pT[it % 8]
                if cur[0][2] is not None:
                    sbb = sbias[it % 4]
                    ACTV(P, pslot[:].rearrange("p a b -> p (a b)"), sbb[:].rearrange("p a b -> p (a b)"), AF.Exp,
                         [sbb], [pslot], scale=0.125)
                else:
                    pair = C.ps2[itc % 2]
                    ACTV(P, pslot[:].rearrange("p a b -> p (a b)"), pair[:, :], AF.Exp,
                         [cur[0][1], cur[1][1]], [pslot], scale=0.125)
                if kb + 2 < nkb:
                    pend[kb + 2] = qk_mm(kb + 2, it + 2)
                if nkb < 32 or kb % 2 == 1:
                    for _ in range(drate):
                        if pend_tail:
                            pend_tail.pop(0)()
                for m in range(2):
                    MM(P, ps[4 + m], ps[4 + m][:], vt[:, kb, :], pslot[:, m, :], kb == 0, kb == nkb - 1, [vt, pslot])
                MM(P, ps[6], ps[6][:], C.ones1[:], pslot[:, 1, :], kb == 0, kb == nkb - 1, [C.ones1, pslot])
                if kb == 0:
                    CP(P, "dve", al[:, 0, :], pslot[:, 0, :], [pslot], [al])
                else:
                    TT(P, "dve", al[:, 0, :], al[:, 0, :], pslot[:, 0, :], ALU.add, [al, pslot], [al])
                it += 1
            while pend_tail:
                pend_tail.pop(0)()
            for m in range(2):
                CP(P, "dve", osb[m][:], ps[4 + m][:], [ps[4 + m]], [osb[m]])
            CP(P, "dve", l1s[:], ps[6][:], [ps[6]], [l1s])
            MM(P, ps[7], ps[7][:], ones_f[:], al[:, 0, :], True, True, [ones_f, al])
            ob_ = obuf[qt % 2]
            pend_tail.extend([
                lambda: RECIP(P, r0[:], ps[7][:], [ps[7]], [r0]),
                lambda: TT(P, "dve", t0_[:], osb[0][:], r0[:], ALU.mult, [osb[0], r0], [t0_]),
                lambda: RECIP(P, r0[:], l1s[:], [l1s], [r0]),
                lambda: TT(P, "dve", t1_[:], osb[1][:], r0[:], ALU.mult, [osb[1], r0], [t1_]),
                lambda: STT(P, "dve", oo[:], t1_[:], neglam, t0_[:], ALU.mult, ALU.add, [t1_, t0_, sc], [oo]),
                lambda: TT(P, "dve", osq[:], oo[:], oo[:], ALU.mult, [oo], [osq]),
                lambda: MM(P, ps[7], ps[7][:], C.ones_sub[:], osq[:], True, True, [C.ones_sub, osq]),
                lambda: TS(P, "dve", rs[:], ps[7][:], NORM_EPS, None, ALU.add, None, [ps[7]], [rs]),
                lambda: ACTV(P, rs[:], rs[:], AF.Ln, [rs], [rs]),
                lambda: ACTV(P, rs[:], rs[:], AF.Exp, [rs], [rs], scale=-0.5),
                lambda ob_=ob_: STT(P, "dve", ob_[:], oo[:], gsc, rs[:], ALU.mult, ALU.mult, [oo, rs, sc], [ob_]),
                lambda ob_=ob_, h=h, qs=qs: DMA(P, C.yT[2 + h, :, qs], ob_[:], r=[ob_]),
            ])
    while pend_tail:
        pend_tail.pop(0)()


def fnet_phase(C, S, l):
    P, ps = C.P, C.ps
    nb = S // 128
    PC = 4
    Gt = P.sb("Gt", [128, nb, 512], BF16)
    cr = [P.sb(f"fcr{i}", [128, PC, 512], BF16) for i in range(2)]
    sr = [P.sb(f"fsr{i}", [128, PC, 512], BF16) for i in range(2)]
    obuf = [P.sb(f"fob{i}", [128, 512], BF16) for i in range(4)]
    DMA(P, Gt[:], C.G[0:S, :].rearrange("(b p) f -> p b f", p=128), w=[Gt])
    norm = 1.0 / math.sqrt(S * 64.0)
    it = 0
    for kt in range(S // 512):
        ksl = slice(kt * 512, (kt + 1) * 512)
        for pc in range(nb // PC):
            ct, st_ = cr[it % 2], sr[it % 2]
            it += 1
            DMA(P, ct[:], C.dft[S][0, pc * PC:(pc + 1) * PC, :, ksl].rearrange("a p k -> p a k"), w=[ct])
            DMA(P, st_[:], C.dft[S][1, pc * PC:(pc + 1) * PC, :, ksl].rearrange("a p k -> p a k"), w=[st_])
            for a in range(PC):
                tb = pc * PC + a
                for fc in range(2):
                    pb = ps[4 + (kt % 2) * 2 + fc]
                    MM(P, pb, pb[:], Gt[:, tb, fc * 128:(fc + 1) * 128], ct[:, a, :], tb == 0, False, [Gt, ct])
                    MM(P, pb, pb[:], Gt[:, tb, 256 + fc * 128:256 + (fc + 1) * 128], st_[:, a, :], False, tb == nb - 1, [Gt, st_])
            yield
        for fc in range(2):
            pb = ps[4 + (kt % 2) * 2 + fc]
            ob = obuf[(kt % 2) * 2 + fc]
            ACTV(P, ob[:], pb[:], AF.Identity, [pb], [ob], scale=norm)
            DMA(P, C.yT[6 + fc, :, ksl], ob[:], r=[ob])
        yield


def _ws(W, n0, kc):
    sub = W[:, n0:n0 + 128].reshape(kc, 128, 128)
    return np.ascontiguousarray(sub.transpose(1, 0, 2)).reshape(128, kc * 128)


def _col(v):
    return np.ascontiguousarray(v.reshape(-1, 128).T)


def prep_shared(inp, depth, Smax, svars):
    f32 = np.float32
    w8 = np.zeros([depth, NCH8, 128, 1024], f32)
    w22 = np.zeros([depth, 16, 128, DFF], f32)
    wv = np.zeros([depth, 128, 8 * 512], f32)
    wgate = np.zeros([depth, 2, 2, 2, 128, 128], f32)
    small = np.zeros([128, NSMALL], f32)
    for l in range(depth):
        for fi, (g, u, d) in enumerate([("ffn1_wg", "ffn1_wu", "ffn1_wd"), ("ffn2_wg", "ffn2_wu", "ffn2_wd")]):
            base = C_GU1 if fi == 0 else C_GU2
            for j in range(NF):
                w8[l, base + 2 * j] = _ws(inp[g][l], j * 128, 8)
                w8[l, base + 2 * j + 1] = _ws(inp[u][l], j * 128, 8)
            for c in range(8):
                w22[l, fi * 8 + c] = _ws(inp[d][l], c * 128, NF)
        win = inp["w_in"][l]
        cols = [0, 128, 256, 384] + [512 + 128 * h for h in range(4)] + [1024 + 128 * h for h in range(4)] + [2048, 2176]
        for i, c0 in enumerate(cols):
            w8[l, C_IN + i] = _ws(win, c0, 8)
        for i in range(8):
            w8[l, C_OUT + i] = _ws(inp["w_out"][l], i * 128, 8)
        wv[l] = np.ascontiguousarray(win[:, 1536:2048].reshape(8, 128, 512).transpose(1, 0, 2)).reshape(128, 4096)
        for d_ in range(2):
            for ai, nm in enumerate(["rg_wa", "rg_wx"]):
                for c in range(2):
                    for hh in range(2):
                        wgate[l, d_, ai, c, hh * 64:(hh + 1) * 64, hh * 64:(hh + 1) * 64] = inp[nm][l, d_, 2 * c + hh]
        for i in range(3):
            small[:, SM_LNG + (l * 3 + i) * 8: SM_LNG + (l * 3 + i) * 8 + 8] = _col(inp["ln_g"][l, i])
            small[:, SM_LNB + (l * 3 + i) * 8: SM_LNB + (l * 3 + i) * 8 + 8] = _col(inp["ln_b"][l, i])
        for j in range(4):
            small[:, SM_CONVW + (l * 4 + j) * 2: SM_CONVW + (l * 4 + j) * 2 + 2] = _col(inp["conv_w"][l, j])
        small[:, SM_CONVB + l * 2: SM_CONVB + l * 2 + 2] = _col(inp["conv_b"][l])
        for d_ in range(2):
            o = (l * 2 + d_) * 2
            small[:, SM_BA + o: SM_BA + o + 2] = _col(inp["rg_ba"][l, d_])
            small[:, SM_BX + o: SM_BX + o + 2] = _col(inp["rg_bx"][l, d_])
            small[:, SM_LAM + o: SM_LAM + o + 2] = _col(inp["rg_lambda"][l, d_])
        small[:, SM_SUBG + l] = inp["subln_g"][l]
        small[:, SM_LQK + l * 256: SM_LQK + (l + 1) * 256] = inp["lambda_qk"][l].reshape(1, 256)
    c64 = np.zeros([128, 2, 512], np.float64)
    cc = np.arange(64)
    ang = 2 * np.pi * np.outer(cc, cc) / 64.0
    for j in range(2):
        for gl in range(2):
            g = 2 * j + gl
            c64[gl * 64:(gl + 1) * 64, j, g * 64:(g + 1) * 64] = -np.cos(ang)
            c64[gl * 64:(gl + 1) * 64, j, 256 + g * 64:256 + (g + 1) * 64] = np.sin(ang)
    t = np.arange(Smax)
    thi, tlo = (t // 128) * 128.0, (t % 128) * 1.0
    kext = np.zeros([4, 4, Smax], f32)
    qext = np.zeros([2, 4, 4, Smax], f32)
    for h in range(4):
        s8 = 8.0 * SLOPES[h]
        kext[h] = np.stack([s8 * thi, s8 * tlo, np.ones(Smax), np.ones(Smax)])
        qext[0, h] = np.stack([np.ones(Smax), np.ones(Smax), -s8 * thi, -s8 * tlo])
        qext[1, h] = -qext[0, h]
    ik = np.arange(128)[:, None]
    xx = np.arange(896)[None, :]
    dbig = np.abs(xx - 384 - ik).astype(f32)
    kio = np.stack([(t % 64) * 1.0, (t // 64) * 1.0]).astype(f32)
    tcol = np.zeros([128, 2, 2, 64], np.float64)
    for vi, S in enumerate(svars):
        A = S // 128
        for a in range(A):
            tt = 128 * a + np.arange(128)
            tcol[:, vi, 0, a] = ((64 * tt) % S) / S
            tcol[:, vi, 1, a] = tt / S
    return dict(w8=w8, w22=w22, wv=wv, wgate=wgate, small=small, c64=c64.reshape(128, 1024).astype(f32),
                kext=kext, qext=qext, dbig=dbig, kio=kio, tcol=tcol.reshape(128, 256).astype(f32))


def to_fm(x):
    S = x.shape[0]
    return np.ascontiguousarray(x.T).reshape(8, 128, S)


def from_fm(y):
    return np.ascontiguousarray(y.reshape(1024, -1).T)


SEQ_LENS = [4096, 4096, 8192]


def kernel(**inputs):
    inp = {k: np.asarray(v) for k, v in inputs.items()}
    xp, xs = inp["x_prompt"], inp["x_sample"]
    n_cores = 8
    nc = build_program(SEQ_LENS)
    sh = prep_shared(inp, DEPTH, max(SEQ_LENS), sorted(set(SEQ_LENS)))
    in_maps = []
    for c in range(n_cores):
        m = dict(sh)
        m["x0"] = to_fm(xp[2 * c])
        m["x1"] = to_fm(xp[2 * c + 1])
        m["x2"] = to_fm(xs[c]) if c < xs.shape[0] else np.zeros([8, 128, SEQ_LENS[2]], np.float32)
        in_maps.append(m)
    res = run_bass_kernel_spmd(nc, in_maps, core_ids=list(range(n_cores)))
    y_prompt = np.stack([from_fm(res.results[c][f"y{i}"]) for c in range(n_cores) for i in range(2)])
    y_sample = np.stack([from_fm(res.results[c]["y2"]) for c in range(xs.shape[0])])
    return (y_prompt.astype(np.float32), y_sample.astype(np.float32))
```
